# Optimizing a Trainium2 kernel written in Bass

```python
import jax, jax.numpy as jnp
from jax import lax
import numpy as np

D_MODEL = 1024
BATCH = 2
SEQ = 16384
DEPTH = 2

N_META = 16
GRID_W = 64
BRANCH_W = D_MODEL // 2
D_MIX = 3 * BRANCH_W
NORM_EPS = 1e-6
NA_HD = 64
NA_HEADS = BRANCH_W // NA_HD
WIN_R = 8
WIN_C = 16
GLA_HEADS = 4
GLA_DV = BRANCH_W // GLA_HEADS
GLA_DK = GLA_DV // 2
GLA_KW = GLA_HEADS * GLA_DK
GLA_LR = 16
GLA_NORMALIZER = 16.0
GLA_CHUNK = 64
GLA_NORM_EPS = 1e-5
RW_N = 64
RW_HEADS = BRANCH_W // RW_N
RW_W_LORA = 64
RW_A_LORA = 64
RW_V_LORA = 32
RW_GN_EPS = 64e-5
CONV_W = 3
NA_COLS = 4 * BRANCH_W
GLA_COLS = 2 * GLA_KW + 2 * BRANCH_W + 2 * GLA_LR
RW_SHIFT_COLS = 3 * BRANCH_W + 2 * RW_W_LORA + 2 * RW_A_LORA
RW_COLS = RW_SHIFT_COLS + BRANCH_W
P_IN = NA_COLS + GLA_COLS + RW_COLS

kernel_name = 'hymba_na_gla_rwkv7_bidir_encoder'


def rms_norm(x, g, eps=NORM_EPS):
    x32 = x.astype(jnp.float32)
    y = x32 * lax.rsqrt(jnp.mean(x32 * x32, axis=-1, keepdims=True) + eps)
    return (y * g.astype(jnp.float32)).astype(x.dtype)


def _split(z, widths):
    bounds = [int(b) for b in np.cumsum(widths)[:-1]]
    return jnp.split(z, bounds, axis=-1)


def _reorder_bwd(z):
    return jnp.concatenate([z[:, :N_META], jnp.flip(z[:, N_META:], axis=1)], axis=1)


def _to_dirs(z_fwd, z_bwd):
    return jnp.stack([z_fwd, _reorder_bwd(z_bwd)], axis=0)


def _sum_dirs(y):
    return y[0] + _reorder_bwd(y[1])


def centred_dwconv(z, w):
    c = z.shape[-1]
    return lax.conv_general_dilated(
        z, w[:, None, :].astype(z.dtype), window_strides=(1,),
        padding=[(CONV_W // 2, CONV_W // 2)],
        dimension_numbers=('NWC', 'WIO', 'NWC'), feature_group_count=c)


def neighbourhood_attention(q, k, v, rpb):
    B, L, H, hd = q.shape
    T = L - N_META
    rows = T // GRID_W
    wr = min(WIN_R, rows)
    scale = hd ** -0.5
    qm, km, vm = q[:, :N_META], k[:, :N_META], v[:, :N_META]
    to_grid = lambda a: a[:, N_META:].reshape(B, rows, GRID_W, H, hd).transpose(0, 3, 1, 2, 4)
    qg, kg, vg = to_grid(q), to_grid(k), to_grid(v)
    km_h, vm_h = km.transpose(0, 2, 1, 3), vm.transpose(0, 2, 1, 3)
    cols = np.arange(GRID_W)
    cstart = np.clip(cols - WIN_C // 2, 0, GRID_W - WIN_C)
    col_idx = cstart[:, None] + np.arange(WIN_C)[None, :]
    dc = col_idx - cols[:, None] + (WIN_C - 1)

    def row_block(r):
        rs = jnp.clip(r - wr // 2, 0, rows - wr)
        qr = lax.dynamic_index_in_dim(qg, r, axis=2, keepdims=False)
        kb = lax.dynamic_slice_in_dim(kg, rs, wr, axis=2)
        vb = lax.dynamic_slice_in_dim(vg, rs, wr, axis=2)
        kw = kb[:, :, :, col_idx]
        vw = vb[:, :, :, col_idx]
        s_loc = jnp.einsum('bhcd,bhrckd->bhcrk', qr, kw).astype(jnp.float32) * scale
        dr = rs + jnp.arange(wr) - r + (WIN_R - 1)
        bias = rpb[:, dr][:, :, dc]
        s_loc = s_loc + bias.transpose(0, 2, 1, 3)[None].astype(jnp.float32)
        s_meta = jnp.einsum('bhcd,bhmd->bhcm', qr, km_h).astype(jnp.float32) * scale
        s = jnp.concatenate([s_loc.reshape(B, H, GRID_W, wr * WIN_C), s_meta], axis=-1)
        p = jax.nn.softmax(s, axis=-1).astype(v.dtype)
        p_loc = p[..., :wr * WIN_C].reshape(B, H, GRID_W, wr, WIN_C)
        p_meta = p[..., wr * WIN_C:]
        return (jnp.einsum('bhcrk,bhrckd->bhcd', p_loc, vw)
                + jnp.einsum('bhcm,bhmd->bhcd', p_meta, vm_h))

    og = lax.map(row_block, jnp.arange(rows))
    og = og.transpose(1, 0, 3, 2, 4).reshape(B, T, H, hd)
    s_mm = jnp.einsum('bmhd,bnhd->bhmn', qm, km).astype(jnp.float32) * scale
    p_mm = jax.nn.softmax(s_mm, axis=-1).astype(v.dtype)
    om = jnp.einsum('bhmn,bnhd->bmhd', p_mm, vm)
    return jnp.concatenate([om, og], axis=1)


def na_branch(z, rpb):
    B, L, _ = z.shape
    q, k, v, g = _split(z, (BRANCH_W,) * 4)
    heads = lambda a: a.reshape(B, L, NA_HEADS, NA_HD)
    o = neighbourhood_attention(heads(q), heads(k), heads(v), rpb)
    return o.reshape(B, L, BRANCH_W) * jax.nn.silu(g)


def gla_chunked(q, k, v, g):
    Dn, B, L, H, dk = q.shape
    dv = v.shape[-1]
    C = GLA_CHUNK
    pad = (-N_META) % C
    padf = lambda a: jnp.pad(a, ((0, 0), (0, 0), (pad, 0), (0, 0), (0, 0)))
    nc = (L + pad) // C
    ch = lambda a: padf(a).reshape(Dn, B, nc, C, H, a.shape[-1]).transpose(0, 1, 4, 2, 3, 5)
    q, k, v, g = ch(q), ch(k), ch(v), ch(g)
    b = jnp.cumsum(g, axis=-2)
    qe = q * jnp.exp(b)
    ke = k * jnp.exp(-b)
    kd = k * jnp.exp(b[..., -1:, :] - b)
    mask = np.tril(np.ones((C, C), dtype=bool))
    A = jnp.where(mask, jnp.einsum('zbhnid,zbhnjd->zbhnij', qe, ke), 0.0)
    o = jnp.einsum('zbhnij,zbhnjv->zbhniv', A, v)
    dS = jnp.einsum('zbhnjd,zbhnjv->zbhndv', kd, v)
    dec = jnp.exp(b[..., -1, :])

    def step(S, inp):
        dS_n, dec_n = inp
        return dec_n[..., None] * S + dS_n, S

    S0 = jnp.zeros((Dn, B, H, dk, dv), jnp.float32)
    _, S_prev = lax.scan(step, S0, (jnp.moveaxis(dS, 3, 0), jnp.moveaxis(dec, 3, 0)))
    S_prev = jnp.moveaxis(S_prev, 0, 3)
    o = o + jnp.einsum('zbhnid,zbhndv->zbhniv', qe, S_prev)
    o = o.transpose(0, 1, 3, 4, 2, 5).reshape(Dn, B, nc * C, H, dv)
    return o[:, :, pad:]


def gla_branch(z, g_up, g_b, norm_g):
    B, L, _ = z.shape
    f32 = jnp.float32
    q, k, v, g, gd = _split(z.astype(f32), (GLA_KW, GLA_KW, BRANCH_W, BRANCH_W, 2 * GLA_LR))
    logits = jnp.einsum('bldr,drc->dblc', gd.reshape(B, L, 2, GLA_LR), g_up.astype(f32)) + g_b[:, None, None, :]
    gk = jax.nn.log_sigmoid(logits) / GLA_NORMALIZER
    hk = lambda a: a.reshape(a.shape[:-1] + (GLA_HEADS, -1))
    q = q * GLA_DK ** -0.5
    o = gla_chunked(hk(_to_dirs(q, q)), hk(_to_dirs(k, k)), hk(_to_dirs(v, v)),
                    hk(_to_dirs(gk[0], gk[1])))
    o = _sum_dirs(o)
    o = o * lax.rsqrt(jnp.mean(o * o, axis=-1, keepdims=True) + GLA_NORM_EPS) * norm_g
    return (o.reshape(B, L, BRANCH_W) * jax.nn.silu(g)).astype(z.dtype)


def rwkv7_scan(r, w, k, v, a, b):
    def step(S, inp):
        r_t, w_t, k_t, v_t, a_t, b_t = inp
        sa = jnp.einsum('zbhvk,zbhk->zbhv', S, a_t)
        S = (S * w_t[..., None, :] + sa[..., :, None] * b_t[..., None, :]
             + v_t[..., :, None] * k_t[..., None, :])
        return S, jnp.einsum('zbhvk,zbhk->zbhv', S, r_t)

    Dn, B, L, H, N = r.shape
    S0 = jnp.zeros((Dn, B, H, N, N), jnp.float32)
    xs = tuple(jnp.moveaxis(t, 2, 0) for t in (r, w, k, v, a, b))
    _, y = lax.scan(step, S0, xs)
    return jnp.moveaxis(y, 0, 2)


def rwkv7_branch(z, v_first, conv_w, w0, w_up, a0, a_up, k_k, k_a, r_k, ln_g, ln_b, v_mix):
    B, L, _ = z.shape
    f32 = jnp.float32
    zs = centred_dwconv(z[..., :RW_SHIFT_COLS], conv_w).astype(f32)
    gate = z[..., RW_SHIFT_COLS:].astype(f32)
    r, k, v, wd, ad = _split(zs, (BRANCH_W,) * 3 + (2 * RW_W_LORA, 2 * RW_A_LORA))
    w_lora = jnp.einsum('bldr,drc->dblc', jnp.tanh(wd.reshape(B, L, 2, RW_W_LORA)), w_up) + w0[:, None, None, :]
    decay = jnp.exp(-jnp.exp(-jax.nn.softplus(-w_lora) - 0.5))
    alpha = jax.nn.sigmoid(jnp.einsum('bldr,drc->dblc', ad.reshape(B, L, 2, RW_A_LORA), a_up) + a0[:, None, None, :])
    hd = lambda a: a.reshape(a.shape[:-1] + (RW_HEADS, RW_N))
    kk = hd(k * k_k)
    kk = kk / jnp.maximum(jnp.sqrt(jnp.sum(kk * kk, axis=-1, keepdims=True)), 1e-12)
    kk = kk.reshape(B, L, BRANCH_W)
    k_mod = k * (1.0 + (alpha - 1.0) * k_a)
    if v_mix is None:
        v_first = v
    else:
        v0, v_down, v_up = v_mix
        v = v + (v_first - v) * jax.nn.sigmoid(v0 + (v @ v_down) @ v_up)
    y = rwkv7_scan(hd(_to_dirs(r, r)), hd(_to_dirs(decay[0], decay[1])),
                   hd(_to_dirs(k_mod[0], k_mod[1])), hd(_to_dirs(v, v)),
                   hd(_to_dirs(-kk, -kk)), hd(_to_dirs(kk * alpha[0], kk * alpha[1])))
    y = _sum_dirs(y)
    mu = jnp.mean(y, axis=-1, keepdims=True)
    var = jnp.mean(jnp.square(y - mu), axis=-1, keepdims=True)
    y = ((y - mu) * lax.rsqrt(var + RW_GN_EPS)).reshape(B, L, BRANCH_W) * ln_g + ln_b
    k_bonus = 0.5 * (k_mod[0] + k_mod[1])
    bonus = jnp.sum(hd(r * k_bonus * r_k), axis=-1, keepdims=True) * hd(v)
    y = y + bonus.reshape(B, L, BRANCH_W)
    return (y * jax.nn.silu(gate)).astype(z.dtype), v_first


def setup_inputs(seed: int = 0) -> dict:
    key = jax.random.key(seed)
    ks = jax.random.split(key, 24)
    nrm = lambda k, shape, s: jax.random.normal(k, shape, jnp.float32) * s
    dm1 = DEPTH - 1
    conv_base = jnp.array([0.25, 0.5, 0.25], jnp.float32)[None, :, None]
    return {
        'x': nrm(ks[0], (BATCH, SEQ, D_MODEL), 1.0),
        'meta': nrm(ks[1], (N_META, D_MODEL), 1.0),
        'norm_g': 1.0 + nrm(ks[2], (DEPTH, D_MODEL), 0.02),
        'w_in': nrm(ks[3], (DEPTH, D_MODEL, P_IN), D_MODEL ** -0.5),
        'w_out': nrm(ks[4], (DEPTH, D_MIX, D_MODEL), D_MIX ** -0.5),
        'na_rpb': nrm(ks[5], (DEPTH, NA_HEADS, 2 * WIN_R - 1, 2 * WIN_C - 1), 0.1),
        'gla_g_up': nrm(ks[6], (DEPTH, 2, GLA_LR, GLA_KW), GLA_LR ** -0.5),
        'gla_g_b': nrm(ks[7], (DEPTH, 2, GLA_KW), 0.1),
        'gla_norm_g': 1.0 + nrm(ks[8], (DEPTH, GLA_DV), 0.02),
        'rw_conv': conv_base + nrm(ks[9], (DEPTH, CONV_W, RW_SHIFT_COLS), 0.05),
        'rw_w0': jax.random.uniform(ks[10], (DEPTH, 2, BRANCH_W), jnp.float32, -4.0, 1.0),
        'rw_w_up': nrm(ks[11], (DEPTH, 2, RW_W_LORA, BRANCH_W), 0.5 * RW_W_LORA ** -0.5),
        'rw_a0': nrm(ks[12], (DEPTH, 2, BRANCH_W), 0.1),
        'rw_a_up': nrm(ks[13], (DEPTH, 2, RW_A_LORA, BRANCH_W), 0.5 * RW_A_LORA ** -0.5),
        'rw_k_k': 0.85 + nrm(ks[14], (DEPTH, BRANCH_W), 0.05),
        'rw_k_a': 1.0 + nrm(ks[15], (DEPTH, BRANCH_W), 0.05),
        'rw_r_k': nrm(ks[16], (DEPTH, BRANCH_W), 0.1),
        'rw_ln_g': 1.0 + nrm(ks[17], (DEPTH, BRANCH_W), 0.02),
        'rw_ln_b': nrm(ks[18], (DEPTH, BRANCH_W), 0.02),
        'rw_v0': 1.0 + nrm(ks[19], (dm1, BRANCH_W), 0.1),
        'rw_v_down': nrm(ks[20], (dm1, BRANCH_W, RW_V_LORA), BRANCH_W ** -0.5),
        'rw_v_up': nrm(ks[21], (dm1, RW_V_LORA, BRANCH_W), 0.5 * RW_V_LORA ** -0.5),
        'final_norm_g': 1.0 + nrm(ks[22], (D_MODEL,), 0.02),
    }


def reference(x, meta, norm_g, w_in, w_out, na_rpb, gla_g_up, gla_g_b, gla_norm_g,
              rw_conv, rw_w0, rw_w_up, rw_a0, rw_a_up, rw_k_k, rw_k_a, rw_r_k,
              rw_ln_g, rw_ln_b, rw_v0, rw_v_down, rw_v_up, final_norm_g):
    B, _, D = x.shape
    h = jnp.concatenate([jnp.broadcast_to(meta[None].astype(x.dtype), (B, N_META, D)), x], axis=1)
    v_first = None
    for l in range(DEPTH):
        hn = rms_norm(h, norm_g[l])
        w = w_in[l]
        z_na = hn @ w[:, :NA_COLS]
        z_gla = hn @ w[:, NA_COLS:NA_COLS + GLA_COLS]
        z_rw = hn @ w[:, NA_COLS + GLA_COLS:]
        o_na = na_branch(z_na, na_rpb[l])
        o_gla = gla_branch(z_gla, gla_g_up[l], gla_g_b[l], gla_norm_g[l])
        v_mix = None if l == 0 else (rw_v0[l - 1], rw_v_down[l - 1], rw_v_up[l - 1])
        o_rw, v_first = rwkv7_branch(z_rw, v_first, rw_conv[l], rw_w0[l], rw_w_up[l], rw_a0[l],
                                     rw_a_up[l], rw_k_k[l], rw_k_a[l], rw_r_k[l],
                                     rw_ln_g[l], rw_ln_b[l], v_mix)
        h = h + jnp.concatenate([o_na, o_gla, o_rw], axis=-1) @ w_out[l]
    return rms_norm(h, final_norm_g)[:, N_META:]
```

```python
import numpy as np
from contextlib import ExitStack
import concourse.bass as bass
import concourse.mybir as mybir
from concourse.bass_utils import run_bass_kernel_spmd

F32 = mybir.dt.float32
BF16 = mybir.dt.bfloat16
AF = mybir.ActivationFunctionType
ALU = mybir.AluOpType

DM = 1024
KC = 8
SEM_LIMIT = 28000

CM_MF2, CM_MB2, CM_SL2, CM_SU2, CM_GMIX, CM_GFF, CM_ID, CM_BO, CM_PAD, CM_ONE = 0, 512, 1024, 1280, 1536, 1792, 2048, 2176, 2304, 2432
CM_W = 2560


def const_masks():
    i = np.arange(128)
    SU = (i[:, None] < i[None, :]).astype(np.float32)
    IU = (i[:, None] <= i[None, :]).astype(np.float32)
    SL = (i[:, None] > i[None, :]).astype(np.float32)
    IL = (i[:, None] >= i[None, :]).astype(np.float32)
    ID = np.eye(128, dtype=np.float32)
    BO = np.zeros((128, 128), np.float32)
    BO[:64, :64] = 1
    BO[64:, 64:] = 1
    PAD = np.zeros((128, 128), np.float32)
    PAD[:, 112:] = 1
    ONE = np.ones((128, 128), np.float32)
    return np.concatenate([SU, IU, SU, IU, SL, IL, SL, IL, SL, SL, SU, SU, IU, IL, IU, IU, ID, BO, PAD, ONE], axis=1)


class Buf:
    __slots__ = ("last_w", "readers")

    def __init__(self):
        self.last_w = None
        self.readers = {}


class V:
    __slots__ = ("ap", "b")

    def __init__(self, ap, b):
        self.ap = ap
        self.b = b


class TT:
    def __init__(self, h, b=None):
        self.h = h
        self.b = b if b is not None else Buf()

    def __getitem__(self, idx):
        return V(self.h[idx], self.b)

    def v3(self, base, nblk, stride, lo, hi, p0=None, p1=None):
        full = self.h[:, :] if p0 is None else self.h[p0:p1, :]
        Wd = full.shape[-1]
        assert Wd % stride == 0 and (base % stride) + hi <= stride, (Wd, base, stride, hi)
        a0, r = base // stride, base % stride
        ap = full.rearrange("p (a b) -> p a b", b=stride)[:, a0:a0 + nblk, r + lo:r + hi]
        return V(ap, self.b)

    def v3p(self, p0, p1, base, nblk, stride, lo, hi):
        return self.v3(base, nblk, stride, lo, hi, p0, p1)


class DT:
    def __init__(self, ap):
        self.ap = ap
        self.bufs = {}

    def v(self, ap, key):
        b = self.bufs.get(key)
        if b is None:
            b = self.bufs[key] = Buf()
        return V(ap, b)


class SemSlot:
    def __init__(self, prog, name, ekey):
        self.prog, self.name, self.ekey = prog, name, ekey
        self.sem = None
        self.cnt = 0
        self.n = 0

    def _roll(self, inc):
        if self.sem is None or self.cnt + inc > SEM_LIMIT:
            self.sem = self.prog.new_sem(f"{self.name}_{self.n}")
            self.n += 1
            self.cnt = 0

    def bump(self, inc):
        self._roll(inc)
        self.cnt += inc
        return (self.sem, self.cnt, self.ekey)

    def peek(self, inc):
        self._roll(inc)
        return (self.sem, self.cnt + inc, self.ekey)

    def cur(self):
        return (self.sem, self.cnt, self.ekey)


class Prog:
    ENG = ("pe", "dve", "act", "pool", "sp")

    def __init__(self, nc, es, ndma=12):
        self.nc, self.es = nc, es
        self.streams = {e: [] for e in self.ENG}
        self.slot = {e: SemSlot(self, "s" + e, e) for e in self.ENG}
        self.seen = {e: {} for e in self.ENG}
        self.dslots = [SemSlot(self, f"d{i}", "dma") for i in range(ndma)]
        self.drr = 0
        self.ninstr = {e: 0 for e in self.ENG}
        self.nsem = 0
        self.pe_serial = False

    def new_sem(self, name):
        self.nsem += 1
        return self.es.enter_context(self.nc.semaphore(name))

    def _wait(self, e, tok):
        sem, val, _ = tok
        k = id(sem)
        if self.seen[e].get(k, 0) >= val:
            return
        self.seen[e][k] = val
        self.streams[e].append(lambda eng, sem=sem, val=val: eng.wait_ge(sem, val))

    def _deps(self, e, reads, writes):
        for b in reads:
            if b.last_w is not None:
                self._wait(e, b.last_w)
        for b in writes:
            if b.last_w is not None and not (e == "pe" and b.last_w[2] == e):
                self._wait(e, b.last_w)
            for r in b.readers.values():
                if not (e == "pe" and r[2] == e):
                    self._wait(e, r)

    def _mark(self, tok, reads, writes):
        for b in reads:
            b.readers[(id(tok[0]), tok[2])] = tok
        for b in writes:
            b.last_w = tok
            b.readers = {}

    def op(self, e, fn, reads=(), writes=(), inc=True, serial=False):
        if e == "pe":
            inc = True
            if (serial or self.pe_serial) and self.slot["pe"].sem is not None and self.slot["pe"].cnt > 0:
                self._wait("pe", self.slot["pe"].cur())
            self.pe_serial = serial
        self._deps(e, reads, writes)
        self.ninstr[e] += 1
        if inc:
            tok = self.slot[e].bump(1)
            sem = tok[0]
            self.streams[e].append(lambda eng, fn=fn, sem=sem: fn(eng).then_inc(sem, 1))
        else:
            tok = self.slot[e].peek(1)
            self.streams[e].append(lambda eng, fn=fn: fn(eng))
        self._mark(tok, reads, writes)
        return tok

    def dma(self, out, in_, q="sp", **kw):
        reads, writes = [in_.b], [out.b]
        self._deps(q, reads, writes)
        s = self.dslots[self.drr]
        self.drr = (self.drr + 1) % len(self.dslots)
        if s.sem is not None and s.cnt > 0:
            self._wait(q, s.cur())
        tok = s.bump(16)
        sem = tok[0]
        self.ninstr[q] += 1
        oa, ia = out.ap, in_.ap
        self.streams[q].append(lambda eng, oa=oa, ia=ia, sem=sem, kw=kw: eng.dma_start(out=oa, in_=ia, **kw).then_inc(sem, 16))
        self._mark(tok, reads, writes)
        return tok

    def collective(self, kind, groups, in_v, out_v):
        q = "pool"
        reads, writes = [in_v.b], [out_v.b]
        self._deps(q, reads, writes)
        s = self.dslots[self.drr]
        self.drr = (self.drr + 1) % len(self.dslots)
        if s.sem is not None and s.cnt > 0:
            self._wait(q, s.cur())
        tok = s.bump(16)
        sem = tok[0]
        ia, oa = in_v.ap, out_v.ap
        self.streams[q].append(lambda eng: eng.collective_compute(kind, ALU.bypass, replica_groups=groups, ins=[ia], outs=[oa]).then_inc(sem, 16))
        self._mark(tok, reads, writes)

    def barrier(self):
        toks = [self.slot[e].cur() for e in self.ENG if self.slot[e].sem is not None and self.slot[e].cnt > 0]
        toks += [s.cur() for s in self.dslots if s.sem is not None and s.cnt > 0]
        for e in self.ENG:
            for t in toks:
                self._wait(e, t)

    def finish(self, bufs, q="sp"):
        for b in bufs:
            if b.last_w is not None:
                self._wait(q, b.last_w)

    def emit(self, block):
        st = self.streams

        @block.tensor
        def _(eng):
            for f in st["pe"]:
                f(eng)

        @block.vector
        def _(eng):
            for f in st["dve"]:
                f(eng)

        @block.scalar
        def _(eng):
            for f in st["act"]:
                f(eng)

        @block.gpsimd
        def _(eng):
            for f in st["pool"]:
                f(eng)

        @block.sync
        def _(eng):
            for f in st["sp"]:
                f(eng)


def _rw(ins, outs):
    r, w = [], []
    for x in ins:
        if isinstance(x, V) and x.b not in r:
            r.append(x.b)
    for x in outs:
        if isinstance(x, V) and x.b not in w:
            w.append(x.b)
    return r, w


def _a(x):
    return x.ap if isinstance(x, V) else x


class Ops:
    def __init__(self, P):
        self.P = P

    def act(self, out, in_, func, bias=None, scale=None, accum=None):
        r, w = _rw([in_, bias, scale], [out, accum])
        kw = {}
        if bias is not None:
            kw["bias"] = _a(bias)
        if scale is not None:
            kw["scale"] = _a(scale)
        if accum is not None:
            kw["accum_out"] = _a(accum)
        oa, ia = out.ap, in_.ap
        self.P.op("act", lambda e: e.activation(out=oa, in_=ia, func=func, **kw), r, w)

    def copy(self, eng, out, in_):
        r, w = _rw([in_], [out])
        oa, ia = out.ap, in_.ap
        if eng == "act":
            self.P.op("act", lambda e: e.activation(out=oa, in_=ia, func=AF.Copy), r, w)
        else:
            self.P.op(eng, lambda e: e.tensor_copy(out=oa, in_=ia), r, w)

    def memset(self, eng, out, val):
        r, w = _rw([], [out])
        oa = out.ap
        self.P.op(eng, lambda e: e.memset(oa, val), r, w)

    def tt(self, eng, out, in0, in1, op):
        r, w = _rw([in0, in1], [out])
        oa, a0, a1 = out.ap, in0.ap, in1.ap
        self.P.op(eng, lambda e: e.tensor_tensor(out=oa, in0=a0, in1=a1, op=op), r, w)

    def ts(self, eng, out, in0, s1, op0, s2=None, op1=None):
        r, w = _rw([in0, s1, s2], [out])
        oa, a0, x1, x2 = out.ap, in0.ap, _a(s1), _a(s2)
        if op1 is None:
            self.P.op(eng, lambda e: e.tensor_scalar(out=oa, in0=a0, scalar1=x1, scalar2=None, op0=op0), r, w)
        else:
            self.P.op(eng, lambda e: e.tensor_scalar(out=oa, in0=a0, scalar1=x1, scalar2=x2, op0=op0, op1=op1), r, w)

    def stt(self, eng, out, in0, scalar, in1, op0, op1):
        r, w = _rw([in0, scalar, in1], [out])
        oa, a0, sc, a1 = out.ap, in0.ap, _a(scalar), in1.ap
        self.P.op(eng, lambda e: e.scalar_tensor_tensor(out=oa, in0=a0, scalar=sc, in1=a1, op0=op0, op1=op1), r, w)

    def scan(self, out, d0, d1):
        r, w = _rw([d0, d1], [out])
        oa, a0, a1 = out.ap, d0.ap, d1.ap
        self.P.op("dve", lambda e: e.tensor_tensor_scan(out=oa, data0=a0, data1=a1, initial=0.0, op0=ALU.mult, op1=ALU.add), r, w)

    def recip(self, out, in_):
        r, w = _rw([in_], [out])
        oa, ia = out.ap, in_.ap
        self.P.op("dve", lambda e: e.reciprocal(out=oa, in_=ia), r, w)

    def mm(self, out, lhsT, rhs, start=True, stop=True, inc=True):
        r, w = _rw([lhsT, rhs], [out])
        oa, la, ra = out.ap, lhsT.ap, rhs.ap
        serial = la.shape[0] < 128
        self.P.op("pe", lambda e: e.matmul(oa, lhsT=la, rhs=ra, start=start, stop=stop), r, w, inc=inc, serial=serial)

    def tr(self, out, in_, ident, inc=True):
        r, w = _rw([in_, ident], [out])
        oa, ia, da = out.ap, in_.ap, ident.ap
        self.P.op("pe", lambda e: e.transpose(out=oa, in_=ia, identity=da), r, w, inc=inc)

    def dma(self, out, in_, **kw):
        self.P.dma(out, in_, **kw)


class RR:
    def __init__(self, items):
        self.items = items
        self.i = 0

    def next(self):
        x = self.items[self.i]
        self.i = (self.i + 1) % len(self.items)
        return x


NAQ, NAK, NAG, GQ, GK, GG, GD, RR_, RK, RV, RWD, RAD, RG = range(13)
NFM = (13, 16)
SHIFT_CTS = ((RR_, RK, RV, RWD, RAD), (RR_, RK, RV, RWD, RAD, 13, 14, 15))
NCW_MAX = 16 * 128 + 256
PVN = ["cw%d_%d" % (j, t) for j in range(8) for t in range(3)] + [
    "w0_0", "w0_1", "a0_0", "a0_1", "k_k", "k_a", "r_k", "ln_g", "ln_b", "v0", "g_b", "gng", "negb", "padrow", "omka"]
PV = {n: i for i, n in enumerate(PVN)}
NPV = len(PVN)

NA_B, GLA_B, RW_B = 0, 2048, 2048 + 1568


def core_cols(l, q):
    a = np.arange
    fm = [NA_B + 128 * q + a(128), NA_B + 512 + 128 * q + a(128), NA_B + 1536 + 128 * q + a(128),
          GLA_B + 64 * q + np.concatenate([a(64), a(64)]), GLA_B + 256 + 64 * q + np.concatenate([a(64), a(64)]),
          GLA_B + 1024 + 128 * q + a(128), GLA_B + 1536 + np.tile(a(32), 4),
          RW_B + 128 * q + a(128), RW_B + 512 + 128 * q + a(128), RW_B + 1024 + 128 * q + a(128),
          RW_B + 1536 + a(128), RW_B + 1664 + a(128), RW_B + 1792 + 128 * q + a(128)]
    others = [o for o in range(4) if o != q]
    if l == 1:
        fm += [RW_B + 1024 + 128 * o + a(128) for o in others]
    tm = [NA_B + 1024 + 128 * q + a(128), GLA_B + 512 + 128 * q + a(128)]
    return np.concatenate(fm + tm)


def core_params(inp, l, q):
    f = np.float32
    others = [o for o in range(4) if o != q]
    pv = np.zeros((128, NPV), f)
    conv = inp["rw_conv"][l]
    shift_off = [128 * q, 512 + 128 * q, 1024 + 128 * q, 1536, 1664] + [1024 + 128 * o for o in others]
    for j, off in enumerate(shift_off):
        for t in range(3):
            pv[:, PV["cw%d_%d" % (j, t)]] = conv[t, off:off + 128]
    sl = slice(128 * q, 128 * q + 128)
    for d in range(2):
        pv[:, PV["w0_%d" % d]] = inp["rw_w0"][l, d, sl]
        pv[:, PV["a0_%d" % d]] = inp["rw_a0"][l, d, sl]
    for n, k in (("k_k", "rw_k_k"), ("k_a", "rw_k_a"), ("r_k", "rw_r_k"), ("ln_g", "rw_ln_g"), ("ln_b", "rw_ln_b")):
        pv[:, PV[n]] = inp[k][l, sl]
    if l == 1:
        pv[:, PV["v0"]] = inp["rw_v0"][0, sl]
    pv[:64, PV["g_b"]] = inp["gla_g_b"][l, 0, 64 * q:64 * q + 64]
    pv[64:, PV["g_b"]] = inp["gla_g_b"][l, 1, 64 * q:64 * q + 64]
    pv[:, PV["gng"]] = inp["gla_norm_g"][l]
    pv[112:, PV["padrow"]] = 1.0
    out = {"pv": pv}
    out["w"] = np.ascontiguousarray(inp["w_in"][l][:, core_cols(l, q)])
    if l == 0:
        out["w"] = np.concatenate([out["w"][:, :13 * 128], np.zeros((1024, 3 * 128), f), out["w"][:, 13 * 128:]], axis=1)
    out["gbc"] = np.ascontiguousarray(np.broadcast_to(inp["norm_g"][l][None, :], (128, 1024)))
    out["wup"] = np.concatenate([inp["rw_w_up"][l, 0][:, sl], inp["rw_w_up"][l, 1][:, sl]], axis=0)
    out["aup"] = np.concatenate([inp["rw_a_up"][l, 0][:, sl], inp["rw_a_up"][l, 1][:, sl]], axis=0)
    gup = np.zeros((32, 128), f)
    gup[:16, :64] = inp["gla_g_up"][l, 0][:, 64 * q:64 * q + 64]
    gup[16:, 64:] = inp["gla_g_up"][l, 1][:, 64 * q:64 * q + 64]
    out["gup"] = gup
    if l == 1:
        vd = inp["rw_v_down"][0]
        out["vdn"] = np.ascontiguousarray(np.stack([vd[128 * o:128 * o + 128] for o in [q] + others], axis=1))
        out["vup"] = np.ascontiguousarray(inp["rw_v_up"][0][:, sl])
    else:
        out["vdn"] = np.zeros((128, 4, 32), f)
        out["vup"] = np.zeros((32, 128), f)
    out["vdn"] = out["vdn"].reshape(128, 128)
    return out


def na_plan(nreal):
    rows = nreal * 2
    wr = min(8, rows)
    cols = np.arange(64)
    cstart = np.clip(cols - 8, 0, 48)
    cfgs = {}
    tiles = []
    blocks = []
    for B in range(nreal // 2):
        lst = []
        for p in range(rows // 2):
            idx = np.full((128, 256), -1, np.int64)
            anyv = False
            for qr in range(4):
                r = 4 * B + qr
                rs = int(np.clip(r - wr // 2, 0, rows - wr))
                for kr2 in range(2):
                    kr = 2 * p + kr2
                    if not (rs <= kr < rs + wr):
                        continue
                    dr = kr - r + 7
                    for qc in range(64):
                        kcs = cstart[qc] + np.arange(16)
                        idx[kr2 * 64 + kcs, qr * 64 + qc] = dr * 31 + (kcs - qc + 15)
                    anyv = True
            if not anyv:
                continue
            key = idx.tobytes()
            if key not in cfgs:
                cfgs[key] = len(tiles)
                tiles.append(idx)
            lst.append((p, cfgs[key]))
        blocks.append(lst)
    return blocks, tiles


def na_bias_tiles(rpb2, tiles):
    out = np.empty((128, len(tiles), 2, 256), np.float32)
    for c, idx in enumerate(tiles):
        for h in range(2):
            flat = rpb2[h].reshape(-1)
            out[:, c, h, :] = np.where(idx >= 0, flat[np.maximum(idx, 0)], np.float32(-30000.0))
    return out.reshape(128, -1)


TMOFF = 16 * 128
NT = 256
import os
KSTOP = os.environ.get("KSTOP", "")
EM05 = float(np.exp(-0.5))


class Ctx:
    def __init__(self, nreal):
        self.nreal = nreal
        self.nchk = nreal + 1
        self.TP = self.nchk * 128
        self.npq = nreal // 4
        self.ntl = self.npq + 1
        self.nc = bass.Bass("TRN2", target_bir_lowering=False)
        self.es = ExitStack()
        self.P = Prog(self.nc, self.es)
        self.O = Ops(self.P)
        self.blocks, self.na_tiles = na_plan(nreal)
        self.ncfg = len(self.na_tiles)
        self.ext_out = []
        self.arena = None

    def sb(self, name, shape, dt=F32):
        if self.arena is None:
            return TT(self.es.enter_context(self.nc.sbuf_tensor(name, list(shape), dt)))
        k = 1 if dt == BF16 else 0
        ncol = shape[1] + (shape[1] % 2)
        off = self.arena_off[k]
        self.arena_off[k] += ncol
        assert self.arena_off[k] <= self.arena_cols[k], (name, k, self.arena_off[k], self.arena_cols[k])
        return TT(self.arena[k][0:shape[0], off:off + shape[1]])

    def arena_start(self, cols32=13000, cols16=23000):
        self.arena = [self.es.enter_context(self.nc.sbuf_tensor("arena32", [128, cols32], F32)),
                      self.es.enter_context(self.nc.sbuf_tensor("arena16", [128, cols16], BF16))]
        self.arena_cols = [cols32, cols16]
        self.arena_off = [0, 0]
        self.arena_hw = [0, 0]

    def arena_reset(self):
        self.arena_hw = [max(a, b) for a, b in zip(self.arena_hw, self.arena_off)]
        self.arena_off = [0, 0]
        self.P.barrier()

    def ps(self, name, shape, dt=F32):
        return TT(self.es.enter_context(self.nc.psum_tensor(name, list(shape), dt)))

    def dram(self, name, shape, dt, kind="Internal"):
        if kind == "Internal":
            t = self.nc.dram_tensor(name, list(shape), dt)
        else:
            t = self.nc.dram_tensor(name, list(shape), dt, kind=kind)
        return DT(t.ap())

    def setup(self):
        sb, ps = self.sb, self.ps
        self.T0 = ps("T0", [128, 1024], BF16)
        self.T1 = ps("T1", [128, 1024], BF16)
        self.FB = RR([ps("FB%d" % i, [128, 512]) for i in range(3)])
        hb0, hb1 = [], []
        for i in range(3):
            bank = self.es.enter_context(self.nc.psum_tensor("HBK%d" % i, [128, 512], F32))
            bb = Buf()
            hb0.append(TT(bank[:, 0:256], bb))
            hb1.append(TT(bank[:, 256:512], bb))
        self.HB = RR(hb0 + hb1)
        self.cm = sb("cm", [128, CM_W])
        self.identB = sb("identB", [128, 128], BF16)
        self.onesB = sb("onesB", [128, 64], BF16)
        self.zeroB = sb("zeroB", [128, 128], BF16)
        self.cmD = self.dram("cmask", [128, CM_W], F32, "ExternalInput")
        O = self.O
        O.dma(self.cm[:, :], self.cmD.v(self.cmD.ap[:, :], 0))
        O.copy("dve", self.identB[:, :], self.cm[:, CM_ID:CM_ID + 128])
        O.copy("dve", self.onesB[:, :], self.cm[:, CM_ONE:CM_ONE + 64])
        O.memset("pool", self.zeroB[:, :], 0.0)
        self.wstage = RR([sb("wst%d" % i, [128, 1024]) for i in range(2)])

    def cmv(self, off, w):
        return self.cm[:, off:off + w]

    def load_cast(self, dst_tt, dcol0, src_dt, src_ap_fn, ncols, key, rows=128):
        c = 0
        while c < ncols:
            w = min(1024, ncols - c)
            st = self.wstage.next()
            self.O.dma(st[0:rows, 0:w], src_dt.v(src_ap_fn(c, w), key))
            self.O.copy("pool", dst_tt[0:rows, dcol0 + c:dcol0 + c + w], st[0:rows, 0:w])
            c += w

    def setup_M(self):
        sb = self.sb
        self.Min = []
        for l in range(2):
            d = {}
            for n, shp in (("w", [1024, NCW_MAX]), ("gbc", [128, 1024]), ("pv", [128, NPV]), ("wup", [128, 128]),
                           ("aup", [128, 128]), ("gup", [32, 128]), ("vdn", [128, 128]), ("vup", [32, 128]),
                           ("nab", [128, self.ncfg * 512])):
                d[n] = self.dram("%s%d" % (n, l), shp, F32, "ExternalInput")
            self.Min.append(d)
        self.wb = sb("wb", [128, KC * NCW_MAX], BF16)
        self.gbc = sb("gbc", [128, 1024])
        self.pv = sb("pv", [128, NPV])
        self.wupB = sb("wupB", [128, 128], BF16)
        self.aupB = sb("aupB", [128, 128], BF16)
        self.gupF = sb("gupF", [32, 128])
        self.vdnB = sb("vdnB", [128, 128], BF16)
        self.vupB = sb("vupB", [32, 128], BF16)
        self.nab = sb("nab", [128, self.ncfg * 512])
        TP = self.TP
        dr = self.dram
        self.S = {n: dr(n, shp, dt) for n, shp, dt in (
            ("naq", [128, TP], BF16), ("nak", [128, TP], BF16), ("nag", [128, TP], BF16), ("nav", [TP, 128], BF16),
            ("rY1", [128, TP], F32), ("rRT", [128, TP], BF16), ("rPD", [self.nchk, 128, 128], F32),
            ("rBV", [128, TP], F32), ("rSG", [128, TP], BF16),
            ("gY1", [128, TP], F32), ("gQT", [128, TP], BF16), ("gD", [self.nchk, 128, 132], F32), ("gSG", [128, TP], BF16),
            ("vfirst", [128, TP], F32))}
        self.Wp = {}
        for n, shp, dt in (("Sg", [128, 128], F32), ("Sgb", [128, 128], BF16), ("Sf", [128, 64], F32), ("Sfb", [128, 64], BF16),
                           ("Sb", [128, 64], F32), ("Sbb", [128, 64], BF16)):
            self.Wp[n] = sb(n, shp, dt)

    def alloc_sweep1(self):
        sb = self.sb
        self.hx = RR([sb("hx%d" % i, [128, 1024]) for i in range(2)])
        self.junk = sb("junk", [128, 1024], BF16)
        self.hnb = RR([sb("hnb%d" % i, [128, 1024], BF16) for i in range(2)])
        self.hnT = [sb("hnT%d" % i, [128, 2 * 1024], BF16) for i in range(2)]
        self.ssq = RR([sb("ssq%d" % i, [128, 4]) for i in range(2)])
        self.stb = RR([sb("stb%d" % i, [128, 256], BF16) for i in range(4)])
        self.stf = RR([sb("stf%d" % i, [128, 256]) for i in range(4)])
        W = {}
        for n in ("q2", "k2", "lwg", "cumg", "epg", "eng", "erg", "rT", "kT", "vT", "vf", "kr", "tmpa", "tmpb", "inv", "kkn",
                  "lw0", "lw1", "al0", "al1", "km0", "km1", "b0", "b1", "cum0", "cum1", "cx", "eex", "epos", "eneg", "erem"):
            W[n] = sb(n, [128, NT])
        W["gdF"] = sb("gdF", [32, NT])
        for n in ("QhT", "KhTg", "KtTg", "tw", "adb", "vTb", "vo0", "vo1", "vo2", "vdb",
                  "BhT0", "BhT1", "KhT0", "KhT1", "BtT0", "BtT1", "KtT0", "KtT1"):
            W[n] = sb(n, [128, NT], BF16)
        W["AR0"] = sb("AR0", [128, 2 * NT], BF16)
        W["AR1"] = sb("AR1", [128, 2 * NT], BF16)
        W["gv"] = sb("gv", [128, NT], BF16)
        W["vtm"] = RR([sb("vtm%d" % i, [128, 256], BF16) for i in range(2)])
        for n in ("totg", "etotg", "tot0", "tot1", "etot0", "etot1"):
            W[n] = sb(n, [128, 4])
        W["zext"] = RR([sb("zext%d" % i, [128, NT + 2]) for i in range(2)])
        W["prevcol"] = sb("prevcol", [128, 8])
        W["hnN"] = sb("hnN", [128, 8], BF16)
        W["nxt"] = sb("nxt", [128, 8])
        W["AT"] = sb("AT", [128, 256], BF16)
        W["ktm"] = sb("ktm", [128, 128], BF16)
        W["gst"] = RR([sb("gst%d" % i, [128, 132]) for i in range(2)])
        W["tk"] = RR([sb("tk%d" % i, [128, 896], BF16) for i in range(2)])
        W["SC1"] = [sb("SC1_%d" % i, [128, 512], BF16) for i in range(2)]
        W["SC2"] = [sb("SC2_%d" % i, [128, 512], BF16) for i in range(2)]
        W["RX"] = [RR([sb("RX%d_%d" % (d, i), [128, 512], BF16) for i in range(2)]) for d in range(2)]
        W["LTn"] = [RR([sb("LTn%d_%d" % (d, i), [128, 256], BF16) for i in range(2)]) for d in range(2)]
        W["RpT"] = [sb("RpT%d" % d, [128, 128], BF16) for d in range(2)]
        W["PD"] = [RR([sb("PD%d_%d" % (d, i), [128, 128]) for i in range(2)]) for d in range(2)]
        W.update(self.Wp)
        self.W = W

    def pvc(self, name):
        c = PV[name]
        return self.pv[:, c:c + 1]

    def load_M_params(self, l):
        O, d = self.O, self.Min[l]
        for kc in range(KC):
            self.load_cast(self.wb, kc * NCW_MAX, d["w"], lambda c, w, kc=kc: d["w"].ap[kc * 128:(kc + 1) * 128, c:c + w], NCW_MAX, 0)
        O.dma(self.gbc[:, :], d["gbc"].v(d["gbc"].ap[:, :], 0))
        O.dma(self.pv[:, :], d["pv"].v(d["pv"].ap[:, :], 0))
        O.dma(self.gupF[:, :], d["gup"].v(d["gup"].ap[:, :], 0))
        O.dma(self.nab[:, :], d["nab"].v(d["nab"].ap[:, :], 0))
        self.load_cast(self.wupB, 0, d["wup"], lambda c, w: d["wup"].ap[:, c:c + w], 128, 0)
        self.load_cast(self.aupB, 0, d["aup"], lambda c, w: d["aup"].ap[:, c:c + w], 128, 0)
        self.load_cast(self.vdnB, 0, d["vdn"], lambda c, w: d["vdn"].ap[:, c:c + w], 128, 0)
        self.load_cast(self.vupB, 0, d["vup"], lambda c, w: d["vup"].ap[:, c:c + w], 128, 0, rows=32)
        O.ts("dve", self.pvc("negb"), self.pvc("g_b"), -1.0, ALU.mult)
        O.ts("dve", self.pvc("omka"), self.pvc("k_a"), -1.0, ALU.mult, 1.0, ALU.add)

    def st_chunks(self, i):
        if i == 0:
            return [0]
        return [2 * i - 1, 2 * i]

    def stage_A(self, i, hsrc):
        O = self.O
        hT = self.hnT[i % 2]
        for ci, n in enumerate(self.st_chunks(i)):
            hx = self.hx.next()
            O.dma(hx[:, :], hsrc(n))
            ss = self.ssq.next()
            O.act(self.junk[:, :], hx[:, :], AF.Square, accum=ss[:, 0:1])
            O.act(ss[:, 1:2], ss[:, 0:1], AF.Sqrt, bias=1e-6, scale=1.0 / DM)
            O.recip(ss[:, 2:3], ss[:, 1:2])
            hn = self.hnb.next()
            O.stt("dve", hn[:, :], hx[:, :], ss[:, 2:3], self.gbc[:, :], ALU.mult, ALU.mult)
            for kc in range(KC):
                O.tr(self.T0[:, kc * 128:(kc + 1) * 128], hn[:, kc * 128:(kc + 1) * 128], self.identB[:, :], inc=(kc == KC - 1))
            O.copy("act", hT[:, ci * 1024:(ci + 1) * 1024], self.T0[:, :])

    def rhs_h(self, i, kc, nch):
        return self.hnT[i % 2].v3(kc * 128, nch, 1024, 0, 128)

    def inproj_fm(self, i, ct, nch, M=128):
        O = self.O
        ps = self.HB.next()
        for kc in range(KC):
            O.mm(ps.v3p(0, M, 0, nch, 128, 0, 128), self.wb[:, kc * NCW_MAX + ct * 128:kc * NCW_MAX + ct * 128 + M],
                 self.rhs_h(i, kc, nch), start=(kc == 0), stop=(kc == KC - 1), inc=(kc == KC - 1))
        return ps

    def sweep1(self, l, hsrc):
        O, W, S = self.O, self.W, self.S
        cmv, pvc = self.cmv, self.pvc
        nst = self.nreal // 2 + 1
        mult, add, sub = ALU.mult, ALU.add, ALU.subtract
        self.stage_A(0, hsrc)
        O.memset("pool", W["prevcol"][:, :], 0.0)
        O.memset("pool", W["Sg"][:, :], 0.0)
        O.memset("pool", W["Sgb"][:, :], 0.0)
        for i in range(nst):
            chunks = self.st_chunks(i)
            nch = len(chunks)
            nt = nch * 128
            t0 = chunks[0] * 128
            last = (i == nst - 1)
            if not last:
                self.stage_A(i + 1, hsrc)
                O.copy("pool", W["hnN"].v3(0, 8, 1, 0, 1), self.hnT[(i + 1) % 2].v3(0, 8, 128, 0, 1))

            def ck(n, key):
                return (key, n)

            if KSTOP == "A0":
                continue
            for ct, nm in ((NAQ, "naq"), (NAK, "nak"), (NAG, "nag")):
                ps = self.inproj_fm(i, ct, nch)
                st = self.stb.next()
                if ct == NAG:
                    O.act(st[:, 0:nt], ps[:, 0:nt], AF.Silu)
                else:
                    O.copy("act", st[:, 0:nt], ps[:, 0:nt])
                if KSTOP != "A1":
                    O.dma(S[nm].v(S[nm].ap[:, t0:t0 + nt], i), st[:, 0:nt])
            if KSTOP in ("A1", "A2"):
                continue
            for ci, n in enumerate(chunks):
                ps = self.HB.next()
                for kc in range(KC):
                    O.mm(ps[:, 0:256], self.hnT[i % 2][:, ci * 1024 + kc * 128:ci * 1024 + (kc + 1) * 128],
                         self.wb[:, kc * NCW_MAX + TMOFF:kc * NCW_MAX + TMOFF + 256], start=(kc == 0), stop=(kc == KC - 1), inc=(kc == KC - 1))
                vt = W["vtm"].next()
                O.copy("act", vt[:, 0:128], ps[:, 0:128])
                O.copy("act", W["gv"][:, ci * 128:(ci + 1) * 128], ps[:, 128:256])
                O.dma(S["nav"].v(S["nav"].ap[n * 128:(n + 1) * 128, :], n), vt[:, 0:128])

            if KSTOP == "A":
                continue
            ps = self.inproj_fm(i, GQ, nch)
            O.ts("dve", W["q2"][:, 0:nt], ps[:, 0:nt], 0.125, mult)
            ps = self.inproj_fm(i, GK, nch)
            O.copy("act", W["k2"][:, 0:nt], ps[:, 0:nt])
            ps = self.inproj_fm(i, GG, nch)
            st = self.stb.next()
            O.act(st[:, 0:nt], ps[:, 0:nt], AF.Silu)
            O.dma(S["gSG"].v(S["gSG"].ap[:, t0:t0 + nt], i), st[:, 0:nt])
            ps = self.inproj_fm(i, GD, nch, M=32)
            O.copy("act", W["gdF"][0:32, 0:nt], ps[0:32, 0:nt])
            ps = self.HB.next()
            O.mm(ps[:, 0:nt], self.gupF[0:32, :], W["gdF"][0:32, 0:nt])
            O.act(W["epg"][:, 0:nt], ps[:, 0:nt], AF.Exp, bias=pvc("negb"), scale=-1.0)
            O.act(W["eng"][:, 0:nt], W["epg"][:, 0:nt], AF.Ln, bias=1.0)
            if i == 0:
                O.stt("dve", W["lwg"][:, 0:nt], W["eng"][:, 0:nt], -1.0 / 16.0, cmv(CM_PAD, 128), mult, mult)
            else:
                O.ts("dve", W["lwg"][:, 0:nt], W["eng"][:, 0:nt], -1.0 / 16.0, mult)
            for ci in range(nch):
                O.scan(W["cumg"][:, ci * 128:(ci + 1) * 128], cmv(CM_ONE, 128), W["lwg"][:, ci * 128:(ci + 1) * 128])
            O.copy("dve", W["totg"].v3(0, nch, 1, 0, 1), W["cumg"].v3(0, nch, 128, 127, 128))
            if i > 0:
                for ci in range(nch):
                    cs = slice(ci * 128, (ci + 1) * 128)
                    O.stt("dve", W["cumg"][64:128, cs], W["lwg"][64:128, cs], W["totg"][64:128, ci:ci + 1], W["cumg"][64:128, cs], add, sub)
            O.act(W["epg"][:, 0:nt], W["cumg"][:, 0:nt], AF.Exp)
            O.act(W["eng"][:, 0:nt], W["cumg"][:, 0:nt], AF.Exp, scale=-1.0)
            for ci in range(nch):
                cs = slice(ci * 128, (ci + 1) * 128)
                O.act(W["erg"][:, cs], W["cumg"][:, cs], AF.Exp, bias=W["totg"][:, ci:ci + 1], scale=-1.0)
            O.act(W["etotg"][:, 0:nch], W["totg"][:, 0:nch], AF.Exp)
            O.tt("dve", W["QhT"][:, 0:nt], W["q2"][:, 0:nt], W["epg"][:, 0:nt], mult)
            O.tt("pool", W["KhTg"][:, 0:nt], W["k2"][:, 0:nt], W["eng"][:, 0:nt], mult)
            O.tt("pool", W["KtTg"][:, 0:nt], W["k2"][:, 0:nt], W["erg"][:, 0:nt], mult)
            for ci, n in enumerate(chunks if KSTOP != "GE" else []):
                cs = slice(ci * 128, (ci + 1) * 128)
                ps = self.HB.next()
                for d in range(2):
                    O.mm(ps[:, d * 128:(d + 1) * 128], W["KhTg"][d * 64:(d + 1) * 64, cs], W["QhT"][d * 64:(d + 1) * 64, cs], inc=(d == 1))
                O.tt("dve", W["AT"][:, :], ps[:, 0:256], cmv(CM_GFF if n == 0 else CM_GMIX, 256), mult)
                if KSTOP == "G1":
                    continue
                pso = self.HB.next()
                O.mm(pso[:, 0:128], W["gv"][:, cs], W["AT"][:, 0:128], start=True, stop=False, inc=False)
                O.mm(pso[:, 0:128], W["gv"][:, cs], W["AT"][:, 128:256], start=False, stop=(n == 0), inc=(n == 0))
                if n > 0:
                    O.mm(pso[:, 0:128], W["Sgb"][0:64, :], W["QhT"][0:64, cs], start=False, stop=True)
                sf = self.stf.next()
                O.copy("act", sf[:, 0:128], pso[:, 0:128])
                O.dma(S["gY1"].v(S["gY1"].ap[:, n * 128:(n + 1) * 128], n), sf[:, 0:128])
                if KSTOP == "G2":
                    continue
                O.tr(self.T1[:, 896:1024], W["KtTg"][:, cs], self.identB[:, :])
                O.copy("act", W["ktm"][:, :], self.T1[:, 896:1024])
                psd = self.HB.next()
                for d in range(2):
                    O.mm(psd[d * 64:(d + 1) * 64, 0:128], W["ktm"][:, d * 64:(d + 1) * 64], W["gv"][:, cs], inc=(d == 1))
                if KSTOP == "G3":
                    continue
                O.stt("dve", W["Sg"][0:64, :], W["Sg"][0:64, :], W["etotg"][0:64, ci:ci + 1], psd[0:64, 0:128], mult, add)
                O.copy("pool", W["Sgb"][0:64, :], W["Sg"][0:64, :])
                if n == 0:
                    O.copy("act", W["Sg"][64:128, :], psd[64:128, 0:128])
                    O.copy("pool", W["Sgb"][64:128, :], W["Sg"][64:128, :])
                else:
                    g = W["gst"].next()
                    O.copy("act", g[64:128, 0:128], psd[64:128, 0:128])
                    O.copy("pool", g[64:128, 128:129], W["etotg"][64:128, ci:ci + 1])
                    O.dma(S["gD"].v(S["gD"].ap[n, 64:128, 0:129], n), g[64:128, 0:129])
                    O.dma(S["gQT"].v(S["gQT"].ap[64:128, n * 128:(n + 1) * 128], n), W["QhT"][64:128, cs])

            if KSTOP in ("G", "GE", "G1", "G2", "G3"):
                continue
            shift = SHIFT_CTS[l]
            dsts = {RR_: W["rT"], RK: W["kT"], RV: W["vT"], RWD: W["tmpa"], RAD: W["adb"], 13: W["vo0"], 14: W["vo1"], 15: W["vo2"]}
            psn = None
            if not last:
                psn = self.HB.next()
                for j, ct in enumerate(shift):
                    for kc in range(KC):
                        O.mm(psn[:, j:j + 1], self.wb[:, kc * NCW_MAX + ct * 128:kc * NCW_MAX + (ct + 1) * 128], W["hnN"][:, kc:kc + 1],
                             start=(kc == 0), stop=(kc == KC - 1), inc=(kc == KC - 1))
                O.copy("act", W["nxt"][:, 0:8], psn[:, 0:8])
            for j, ct in enumerate(shift):
                ps = self.inproj_fm(i, ct, nch)
                z = W["zext"].next()
                O.copy("act", z[:, 1:nt + 1], ps[:, 0:nt])
                O.copy("pool", z[:, 0:1], W["prevcol"][:, j:j + 1])
                if last:
                    O.memset("pool", z[:, nt + 1:nt + 2], 0.0)
                else:
                    O.copy("pool", z[:, nt + 1:nt + 2], W["nxt"][:, j:j + 1])
                O.copy("pool", W["prevcol"][:, j:j + 1], z[:, nt:nt + 1])
                dst = dsts[ct]
                cw = [self.pv[:, PV["cw%d_%d" % (j, t)]:PV["cw%d_%d" % (j, t)] + 1] for t in range(3)]
                O.ts("dve", W["tmpb"][:, 0:nt], z[:, 0:nt], cw[0], mult)
                O.stt("dve", W["tmpb"][:, 0:nt], z[:, 1:nt + 1], cw[1], W["tmpb"][:, 0:nt], mult, add)
                O.stt("dve", dst[:, 0:nt], z[:, 2:nt + 2], cw[2], W["tmpb"][:, 0:nt], mult, add)
            O.act(W["tw"][:, 0:nt], W["tmpa"][:, 0:nt], AF.Tanh)
            ps = self.inproj_fm(i, RG, nch)
            st = self.stb.next()
            O.act(st[:, 0:nt], ps[:, 0:nt], AF.Silu)
            O.dma(S["rSG"].v(S["rSG"].ap[:, t0:t0 + nt], i), st[:, 0:nt])
            if l == 1:
                O.copy("pool", W["vTb"][:, 0:nt], W["vT"][:, 0:nt])
                ps = self.HB.next()
                srcs = [W["vTb"], W["vo0"], W["vo1"], W["vo2"]]
                for j in range(4):
                    O.mm(ps[0:32, 0:nt], self.vdnB[:, j * 32:(j + 1) * 32], srcs[j][:, 0:nt], start=(j == 0), stop=(j == 3), inc=(j == 3))
                O.copy("act", W["vdb"][0:32, 0:nt], ps[0:32, 0:nt])
                ps = self.HB.next()
                O.mm(ps[:, 0:nt], self.vupB[0:32, :], W["vdb"][0:32, 0:nt])
                O.act(W["tmpa"][:, 0:nt], ps[:, 0:nt], AF.Sigmoid, bias=pvc("v0"))
                O.dma(W["vf"][:, 0:nt], S["vfirst"].v(S["vfirst"].ap[:, t0:t0 + nt], i))
                O.tt("pool", W["vf"][:, 0:nt], W["vf"][:, 0:nt], W["vT"][:, 0:nt], sub)
                O.tt("dve", W["vf"][:, 0:nt], W["vf"][:, 0:nt], W["tmpa"][:, 0:nt], mult)
                O.tt("dve", W["vT"][:, 0:nt], W["vT"][:, 0:nt], W["vf"][:, 0:nt], add)
            else:
                O.dma(S["vfirst"].v(S["vfirst"].ap[:, t0:t0 + nt], i), W["vT"][:, 0:nt])
            O.copy("pool", W["vTb"][:, 0:nt], W["vT"][:, 0:nt])
            O.ts("dve", W["kr"][:, 0:nt], W["kT"][:, 0:nt], pvc("k_k"), mult)
            O.tt("pool", W["tmpa"][:, 0:nt], W["kr"][:, 0:nt], W["kr"][:, 0:nt], mult)
            ps = self.HB.next()
            O.mm(ps[:, 0:nt], cmv(CM_BO, 128), W["tmpa"][:, 0:nt])
            O.act(W["inv"][:, 0:nt], ps[:, 0:nt], AF.Sqrt)
            O.ts("dve", W["inv"][:, 0:nt], W["inv"][:, 0:nt], 1e-12, ALU.max)
            O.recip(W["inv"][:, 0:nt], W["inv"][:, 0:nt])
            O.tt("dve", W["kkn"][:, 0:nt], W["kr"][:, 0:nt], W["inv"][:, 0:nt], mult)
            for d in range(2):
                lw, al, km, bb = W["lw%d" % d], W["al%d" % d], W["km%d" % d], W["b%d" % d]
                hs = slice(d * 64, (d + 1) * 64)
                ps = self.HB.next()
                O.mm(ps[:, 0:nt], self.wupB[hs, :], W["tw"][hs, 0:nt])
                O.act(W["tmpa"][:, 0:nt], ps[:, 0:nt], AF.Sigmoid, bias=pvc("w0_%d" % d))
                if i == 0:
                    O.stt("dve", lw[:, 0:nt], W["tmpa"][:, 0:nt], -EM05, cmv(CM_PAD, 128), mult, mult)
                else:
                    O.ts("pool", lw[:, 0:nt], W["tmpa"][:, 0:nt], -EM05, mult)
                ps = self.HB.next()
                O.mm(ps[:, 0:nt], self.aupB[hs, :], W["adb"][hs, 0:nt])
                O.act(al[:, 0:nt], ps[:, 0:nt], AF.Sigmoid, bias=pvc("a0_%d" % d))
                O.ts("dve", W["tmpb"][:, 0:nt], al[:, 0:nt], pvc("k_a"), mult, pvc("omka"), add)
                O.tt("dve", km[:, 0:nt], W["tmpb"][:, 0:nt], W["kT"][:, 0:nt], mult)
                O.tt("pool", bb[:, 0:nt], W["kkn"][:, 0:nt], al[:, 0:nt], mult)
            O.tt("pool", W["tmpa"][:, 0:nt], W["km0"][:, 0:nt], W["km1"][:, 0:nt], add)
            O.stt("dve", W["tmpa"][:, 0:nt], W["tmpa"][:, 0:nt], pvc("r_k"), W["rT"][:, 0:nt], mult, mult)
            ps = self.HB.next()
            O.mm(ps[:, 0:nt], cmv(CM_BO, 128), W["tmpa"][:, 0:nt])
            sf = self.stf.next()
            O.stt("dve", sf[:, 0:nt], ps[:, 0:nt], 0.5, W["vT"][:, 0:nt], mult, mult)
            O.dma(S["rBV"].v(S["rBV"].ap[:, t0:t0 + nt], i), sf[:, 0:nt])
            for d in range(2):
                lw, km, bb, cum, tot, etot = W["lw%d" % d], W["km%d" % d], W["b%d" % d], W["cum%d" % d], W["tot%d" % d], W["etot%d" % d]
                AR = W["AR%d" % d]
                for ci in range(nch):
                    cs = slice(ci * 128, (ci + 1) * 128)
                    O.scan(cum[:, cs], cmv(CM_ONE, 128), lw[:, cs])
                O.copy("dve", tot.v3(0, nch, 1, 0, 1), cum.v3(0, nch, 128, 127, 128))
                if d == 1 and i > 0:
                    for ci in range(nch):
                        cs = slice(ci * 128, (ci + 1) * 128)
                        O.stt("dve", cum[:, cs], lw[:, cs], tot[:, ci:ci + 1], cum[:, cs], add, sub)
                O.tt("pool", W["cx"][:, 0:nt], cum[:, 0:nt], lw[:, 0:nt], sub)
                O.act(W["eex"][:, 0:nt], W["cx"][:, 0:nt], AF.Exp)
                O.act(W["epos"][:, 0:nt], cum[:, 0:nt], AF.Exp)
                O.act(W["eneg"][:, 0:nt], cum[:, 0:nt], AF.Exp, scale=-1.0)
                for ci in range(nch):
                    cs = slice(ci * 128, (ci + 1) * 128)
                    O.act(W["erem"][:, cs], cum[:, cs], AF.Exp, bias=tot[:, ci:ci + 1], scale=-1.0)
                O.act(etot[:, 0:nch], tot[:, 0:nch], AF.Exp)
                O.stt("dve", AR.v3(0, nch, 256, 0, 128), W["kkn"].v3(0, nch, 128, 0, 128), -1.0, W["eex"].v3(0, nch, 128, 0, 128), mult, mult)
                O.tt("dve", AR.v3(0, nch, 256, 128, 256), W["rT"].v3(0, nch, 128, 0, 128), W["epos"].v3(0, nch, 128, 0, 128), mult)
                O.tt("pool", W["BhT%d" % d][:, 0:nt], bb[:, 0:nt], W["eneg"][:, 0:nt], mult)
                O.tt("dve", W["KhT%d" % d][:, 0:nt], km[:, 0:nt], W["eneg"][:, 0:nt], mult)
                O.tt("pool", W["BtT%d" % d][:, 0:nt], bb[:, 0:nt], W["erem"][:, 0:nt], mult)
                O.tt("dve", W["KtT%d" % d][:, 0:nt], km[:, 0:nt], W["erem"][:, 0:nt], mult)
            for ci, n in enumerate(chunks if KSTOP != "R" else []):
                self.rwkv_chunk(ci, n)

    def rwkv_chunk(self, ci, n):
        O, W, S = self.O, self.W, self.S
        cmv = self.cmv
        mult, add = ALU.mult, ALU.add
        cs = slice(ci * 128, (ci + 1) * 128)
        tk = W["tk"].next()
        for d in range(2):
            O.tr(self.T1[:, (3 * d) * 128:(3 * d + 1) * 128], W["AR%d" % d][:, ci * 256:ci * 256 + 128], self.identB[:, :], inc=False)
            O.tr(self.T1[:, (3 * d + 1) * 128:(3 * d + 2) * 128], W["BtT%d" % d][:, cs], self.identB[:, :], inc=False)
            O.tr(self.T1[:, (3 * d + 2) * 128:(3 * d + 3) * 128], W["KtT%d" % d][:, cs], self.identB[:, :], inc=False)
        O.tr(self.T1[:, 768:896], W["vTb"][:, cs], self.identB[:, :])
        O.copy("act", tk[:, 0:896], self.T1[:, 0:896])
        fin = []
        for d in range(2):
            fwdtype = (d == 0 or n == 0)
            AR, BhT, KhT = W["AR%d" % d], W["BhT%d" % d], W["KhT%d" % d]
            SC1, SC2 = W["SC1"][d], W["SC2"][d]
            RX = W["RX"][d].next()
            ps1, ps2, psl = self.FB.next(), self.FB.next(), self.HB.next()
            for h in range(2):
                hs = slice(h * 64, (h + 1) * 64)
                O.mm(ps1[:, h * 256:(h + 1) * 256], BhT[hs, cs], AR[hs, ci * 256:(ci + 1) * 256], inc=(h == 1))
            for h in range(2):
                hs = slice(h * 64, (h + 1) * 64)
                O.mm(ps2[:, h * 256:(h + 1) * 256], KhT[hs, cs], AR[hs, ci * 256:(ci + 1) * 256], inc=(h == 1))
            for h in range(2):
                hs = slice(h * 64, (h + 1) * 64)
                O.mm(psl[:, h * 128:(h + 1) * 128], AR[hs, ci * 256:ci * 256 + 128], BhT[hs, cs], inc=(h == 1))
            M2 = cmv(CM_MF2 if fwdtype else CM_MB2, 512)
            O.tt("dve", SC1[:, :], ps1[:, :], M2, mult)
            O.tt("dve", SC2[:, :], ps2[:, :], M2, mult)
            LMoff = CM_SL2 if fwdtype else CM_SU2
            O.tt("dve", RX.v3(0, 2, 256, 0, 128), psl.v3(0, 2, 128, 0, 128), self.cm.v3(LMoff, 2, 128, 0, 128), mult)
            O.copy("pool", RX.v3(0, 2, 256, 128, 192), tk.v3(3 * d * 128, 2, 64, 0, 64))
            psw = self.HB.next()
            for h in range(2):
                O.mm(psw[:, h * 64:(h + 1) * 64], SC2[:, h * 256:h * 256 + 128], tk[:, 768 + h * 64:768 + (h + 1) * 64], inc=(h == 1))
            O.copy("act", RX.v3(0, 2, 256, 192, 256), psw.v3(0, 2, 64, 0, 64))
            LT = SC1
            lt_off, lt_stride = 0, 256
            for j in range(7):
                lastl = (j == 6)
                PS = self.FB.next()
                for h in range(2):
                    lt = LT[:, lt_off + h * lt_stride:lt_off + h * lt_stride + 128]
                    if lastl:
                        O.mm(PS[:, h * 256 + 128:(h + 1) * 256], lt, RX[:, h * 256 + 128:(h + 1) * 256], inc=(h == 1))
                    else:
                        O.mm(PS[:, h * 256:(h + 1) * 256], lt, RX[:, h * 256:(h + 1) * 256], inc=(h == 1))
                RXn = W["RX"][d].next()
                if not lastl:
                    PST = self.HB.next()
                    for h in range(2):
                        lt = LT[:, lt_off + h * lt_stride:lt_off + h * lt_stride + 128]
                        O.mm(PST[:, h * 128:(h + 1) * 128], RX[:, h * 256:h * 256 + 128], lt, inc=(h == 1))
                    LTn = W["LTn"][d].next()
                    O.copy("act", LTn[:, :], PST[:, 0:256])
                if j < 5:
                    O.copy("act", RXn.v3(0, 2, 256, 0, 128), PS.v3(0, 2, 256, 0, 128))
                O.tt("dve", RXn.v3(0, 2, 256, 128, 256), PS.v3(0, 2, 256, 128, 256), RX.v3(0, 2, 256, 128, 256), add)
                RX = RXn
                if not lastl:
                    LT, lt_off, lt_stride = LTn, 0, 128
            psr = self.HB.next()
            for h in range(2):
                O.mm(psr[h * 64:(h + 1) * 64, 0:128], RX[:, h * 256 + 128:h * 256 + 192], SC1[:, h * 256 + 128:(h + 1) * 256], inc=(h == 1))
            RpT = W["RpT"][d]
            O.tt("dve", RpT[:, :], psr[:, 0:128], AR[:, ci * 256 + 128:(ci + 1) * 256], add)
            psp = self.HB.next()
            etot = W["etot%d" % d]
            for h in range(2):
                hp = slice(h * 64, (h + 1) * 64)
                Ap = RX[:, h * 256 + 128:h * 256 + 192]
                U0 = RX[:, h * 256 + 192:h * 256 + 256]
                Bt = tk[:, (3 * d + 1) * 128 + h * 64:(3 * d + 1) * 128 + (h + 1) * 64]
                Kt = tk[:, (3 * d + 2) * 128 + h * 64:(3 * d + 2) * 128 + (h + 1) * 64]
                Vh = tk[:, 768 + h * 64:768 + (h + 1) * 64]
                O.mm(psp[hp, 0:64], Ap, Bt, inc=False)
                O.mm(psp[hp, 64:128], Bt, U0, start=True, stop=False, inc=False)
                O.mm(psp[hp, 64:128], Kt, Vh, start=False, stop=True, inc=(h == 1))
            PD = W["PD"][d].next()
            for h in range(2):
                hp = slice(h * 64, (h + 1) * 64)
                O.stt("dve", PD[hp, 0:64], self.cm[hp, CM_ID + h * 64:CM_ID + (h + 1) * 64], etot[hp, ci:ci + 1], psp[hp, 0:64], mult, add)
            O.copy("act", PD[:, 64:128], psp[:, 64:128])
            fin.append((RX, PD, RpT))
        psy = self.HB.next()
        for h in range(2):
            hp = slice(h * 64, (h + 1) * 64)
            Vh = tk[:, 768 + h * 64:768 + (h + 1) * 64]
            for d in range(2):
                RX = fin[d][0]
                U0 = RX[:, h * 256 + 192:h * 256 + 256]
                O.mm(psy[hp, 0:128], U0, W["SC1"][d][:, h * 256 + 128:(h + 1) * 256], start=(d == 0), stop=False, inc=False)
                O.mm(psy[hp, 0:128], Vh, W["SC2"][d][:, h * 256 + 128:(h + 1) * 256], start=False, stop=(d == 1 and n == 0),
                     inc=(d == 1 and n == 0 and h == 1))
            if n > 0:
                O.mm(psy[hp, 0:128], W["Sfb"][hp, 0:64], fin[0][2][hp, :], start=False, stop=True, inc=(h == 1))
        sf = self.stf.next()
        O.copy("act", sf[:, 0:128], psy[:, 0:128])
        O.dma(S["rY1"].v(S["rY1"].ap[:, n * 128:(n + 1) * 128], n), sf[:, 0:128])
        PD0, PD1 = fin[0][1], fin[1][1]
        if n == 0:
            O.copy("dve", W["Sf"][:, :], PD0[:, 64:128])
            O.copy("dve", W["Sb"][:, :], PD1[:, 64:128])
            O.copy("pool", W["Sbb"][:, :], W["Sb"][:, :])
        else:
            pss = self.HB.next()
            for h in range(2):
                hp = slice(h * 64, (h + 1) * 64)
                O.mm(pss[hp, 0:64], PD0[hp, 0:64], W["Sf"][hp, 0:64], inc=(h == 1))
            O.tt("dve", W["Sf"][:, :], pss[:, 0:64], PD0[:, 64:128], add)
            O.dma(S["rPD"].v(S["rPD"].ap[n, :, :], n), PD1[:, :])
            O.dma(S["rRT"].v(S["rRT"].ap[:, n * 128:(n + 1) * 128], n), fin[1][2][:, :])
        O.copy("pool", W["Sfb"][:, :], W["Sf"][:, :])

    def setup_sweep2(self):
        sb = self.sb
        X = {}
        X["y1"] = RR([sb("y1_%d" % i, [128, 128]) for i in range(2)])
        X["g1"] = RR([sb("g1_%d" % i, [128, 128]) for i in range(2)])
        X["rt"] = RR([sb("rt_%d" % i, [128, 128], BF16) for i in range(2)])
        X["pd"] = RR([sb("pd_%d" % i, [128, 128]) for i in range(2)])
        X["gq"] = RR([sb("gq_%d" % i, [128, 128], BF16) for i in range(2)])
        X["gd"] = RR([sb("gd_%d" % i, [128, 132]) for i in range(2)])
        X["yr"] = sb("yr", [128, NT])
        X["yg"] = sb("yg", [128, NT])
        X["bv"] = sb("bv", [128, NT])
        X["sgr"] = sb("sgr", [128, NT], BF16)
        X["sgg"] = sb("sgg", [128, NT], BF16)
        for n in ("f1", "f2", "f3", "f4"):
            X[n] = sb(n, [128, NT])
        X["ob"] = RR([sb("ob%d" % i, [128, NT], BF16) for i in range(2)])
        self.X = X
        self.W = dict(self.Wp)

    def sweep2(self, l, odst):
        O, W, S, X = self.O, self.W, self.S, self.X
        cmv, pvc = self.cmv, self.pvc
        mult, add, sub = ALU.mult, ALU.add, ALU.subtract
        nst = self.nreal // 2 + 1
        for i in list(range(nst - 1, 0, -1)) + [0]:
            chunks = self.st_chunks(i)
            nch = len(chunks)
            nt = nch * 128
            t0 = chunks[0] * 128
            O.dma(X["bv"][:, 0:nt], S["rBV"].v(S["rBV"].ap[:, t0:t0 + nt], i))
            O.dma(X["sgr"][:, 0:nt], S["rSG"].v(S["rSG"].ap[:, t0:t0 + nt], i))
            O.dma(X["sgg"][:, 0:nt], S["gSG"].v(S["gSG"].ap[:, t0:t0 + nt], i))
            for ci in reversed(range(nch)):
                n = chunks[ci]
                cs = slice(ci * 128, (ci + 1) * 128)
                y1, g1 = X["y1"].next(), X["g1"].next()
                O.dma(y1[:, :], S["rY1"].v(S["rY1"].ap[:, n * 128:(n + 1) * 128], n))
                O.dma(g1[:, :], S["gY1"].v(S["gY1"].ap[:, n * 128:(n + 1) * 128], n))
                if n == 0:
                    O.copy("pool", X["yr"][:, cs], y1[:, :])
                    O.copy("pool", X["yg"][:, cs], g1[:, :])
                    continue
                rt, pd, gq, gd = X["rt"].next(), X["pd"].next(), X["gq"].next(), X["gd"].next()
                O.dma(rt[:, :], S["rRT"].v(S["rRT"].ap[:, n * 128:(n + 1) * 128], n))
                O.dma(pd[:, :], S["rPD"].v(S["rPD"].ap[n, :, :], n))
                O.dma(gq[64:128, :], S["gQT"].v(S["gQT"].ap[64:128, n * 128:(n + 1) * 128], n))
                O.dma(gd[64:128, 0:129], S["gD"].v(S["gD"].ap[n, 64:128, 0:129], n))
                psy = self.HB.next()
                for h in range(2):
                    hp = slice(h * 64, (h + 1) * 64)
                    O.mm(psy[hp, 0:128], W["Sbb"][hp, 0:64], rt[hp, :], inc=(h == 1))
                O.tt("dve", X["yr"][:, cs], psy[:, 0:128], y1[:, :], add)
                pss = self.HB.next()
                for h in range(2):
                    hp = slice(h * 64, (h + 1) * 64)
                    O.mm(pss[hp, 0:64], pd[hp, 0:64], W["Sb"][hp, 0:64], inc=(h == 1))
                O.tt("dve", W["Sb"][:, :], pss[:, 0:64], pd[:, 64:128], add)
                O.copy("pool", W["Sbb"][:, :], W["Sb"][:, :])
                pso = self.HB.next()
                O.mm(pso[:, 0:128], W["Sgb"][64:128, :], gq[64:128, :])
                O.tt("dve", X["yg"][:, cs], pso[:, 0:128], g1[:, :], add)
                O.stt("dve", W["Sg"][64:128, :], W["Sg"][64:128, :], gd[64:128, 128:129], gd[64:128, 0:128], mult, add)
                O.copy("pool", W["Sgb"][64:128, :], W["Sg"][64:128, :])
            yr, f1, f2, f3 = X["yr"], X["f1"], X["f2"], X["f3"]
            ps = self.HB.next()
            O.mm(ps[:, 0:nt], cmv(CM_BO, 128), yr[:, 0:nt])
            O.stt("dve", f1[:, 0:nt], ps[:, 0:nt], -1.0 / 64.0, yr[:, 0:nt], mult, add)
            O.tt("pool", f2[:, 0:nt], f1[:, 0:nt], f1[:, 0:nt], mult)
            ps = self.HB.next()
            O.mm(ps[:, 0:nt], cmv(CM_BO, 128), f2[:, 0:nt])
            O.act(f3[:, 0:nt], ps[:, 0:nt], AF.Sqrt, bias=64e-5, scale=1.0 / 64.0)
            O.recip(f3[:, 0:nt], f3[:, 0:nt])
            O.tt("dve", f1[:, 0:nt], f1[:, 0:nt], f3[:, 0:nt], mult)
            O.ts("dve", f1[:, 0:nt], f1[:, 0:nt], pvc("ln_g"), mult, pvc("ln_b"), add)
            O.tt("pool", f1[:, 0:nt], f1[:, 0:nt], X["bv"][:, 0:nt], add)
            ob = X["ob"].next()
            O.tt("dve", ob[:, 0:nt], f1[:, 0:nt], X["sgr"][:, 0:nt], mult)
            for ci, n in enumerate(chunks):
                for dv in odst(2, n):
                    O.dma(dv, ob[:, ci * 128:(ci + 1) * 128])
            yg, f4 = X["yg"], X["f4"]
            O.tt("pool", f2[:, 0:nt], yg[:, 0:nt], yg[:, 0:nt], mult)
            ps = self.HB.next()
            O.mm(ps[:, 0:nt], cmv(CM_ONE, 128), f2[:, 0:nt])
            O.act(f4[:, 0:nt], ps[:, 0:nt], AF.Sqrt, bias=1e-5, scale=1.0 / 128.0)
            O.recip(f4[:, 0:nt], f4[:, 0:nt])
            O.tt("dve", f4[:, 0:nt], f4[:, 0:nt], yg[:, 0:nt], mult)
            ob = X["ob"].next()
            O.stt("dve", ob[:, 0:nt], f4[:, 0:nt], pvc("gng"), X["sgg"][:, 0:nt], mult, mult)
            for ci, n in enumerate(chunks):
                for dv in odst(1, n):
                    O.dma(dv, ob[:, ci * 128:(ci + 1) * 128])

    def alloc_NA(self):
        sb = self.sb
        N = {}
        N["QT"] = RR([sb("naQT%d" % i, [128, 256], BF16) for i in range(2)])
        N["sg"] = RR([sb("nasg%d" % i, [128, 256], BF16) for i in range(2)])
        N["KT"] = RR([sb("naKT%d" % i, [128, 768], BF16) for i in range(2)])
        N["Vt"] = RR([sb("naVt%d" % i, [128, 768], BF16) for i in range(2)])
        N["KmT"] = sb("naKmT", [128, 16], BF16)
        N["Vm"] = sb("naVm", [16, 128], BF16)
        N["tmp"] = RR([sb("natmp%d" % i, [128, 512]) for i in range(2)])
        N["PT"] = RR([sb("naPT%d" % i, [128, 512], BF16) for i in range(3)])
        N["PmT"] = sb("naPmT", [16, 512], BF16)
        N["rec"] = sb("narec", [128, 256])
        N["of"] = sb("naof", [128, 256])
        N["ob"] = RR([sb("naob%d" % i, [128, 256], BF16) for i in range(2)])
        self.N = N

    def na_phase(self, odst):
        O, S, N = self.O, self.S, self.N
        mult, add = ALU.mult, ALU.add
        O.dma(N["KmT"][:, :], S["nak"].v(S["nak"].ap[:, 112:128], "na"))
        O.dma(N["Vm"][:, :], S["nav"].v(S["nav"].ap[112:128, :], "na"))
        QT, sg = N["QT"].next(), N["sg"].next()
        O.dma(QT[:, 0:16], S["naq"].v(S["naq"].ap[:, 112:128], "na"))
        O.dma(sg[:, 0:16], S["nag"].v(S["nag"].ap[:, 112:128], "na"))
        psm = self.FB.next()
        for h in range(2):
            hs = slice(h * 64, (h + 1) * 64)
            O.mm(psm[0:16, h * 16:(h + 1) * 16], N["KmT"][hs, 0:16], QT[hs, 0:16], inc=(h == 1))
        O.act(N["PmT"][0:16, 0:32], psm[0:16, 0:32], AF.Exp, scale=0.125)
        psO, psD = self.HB.next(), self.HB.next()
        for h in range(2):
            hs = slice(h * 64, (h + 1) * 64)
            O.mm(psO[hs, 0:16], N["Vm"][0:16, hs], N["PmT"][0:16, h * 16:(h + 1) * 16], inc=False)
            O.mm(psD[hs, 0:16], self.onesB[0:16, 0:64], N["PmT"][0:16, h * 16:(h + 1) * 16], inc=(h == 1))
        O.recip(N["rec"][:, 0:16], psD[:, 0:16])
        O.tt("dve", N["of"][:, 0:16], psO[:, 0:16], N["rec"][:, 0:16], mult)
        ob = N["ob"].next()
        O.memset("pool", ob[:, 0:128], 0.0)
        O.tt("pool", ob[:, 112:128], N["of"][:, 0:16], sg[:, 0:16], mult)
        for dv in odst(0, 0):
            O.dma(dv, ob[:, 0:128])
        for B, lst in enumerate(self.blocks):
            c0 = 1 + 2 * B
            QT, sg, KT, Vt = N["QT"].next(), N["sg"].next(), N["KT"].next(), N["Vt"].next()
            O.dma(QT[:, :], S["naq"].v(S["naq"].ap[:, c0 * 128:(c0 + 2) * 128], "na"))
            O.dma(sg[:, :], S["nag"].v(S["nag"].ap[:, c0 * 128:(c0 + 2) * 128], "na"))
            plo, phi = lst[0][0], lst[-1][0]
            nk = phi - plo + 1
            assert nk == len(lst) and nk <= 6
            O.dma(KT[:, 0:nk * 128], S["nak"].v(S["nak"].ap[:, (plo + 1) * 128:(phi + 2) * 128], "na"))
            O.dma(Vt.v3(0, nk, 128, 0, 128),
                  S["nav"].v(S["nav"].ap[(plo + 1) * 128:(phi + 2) * 128, :].rearrange("(n p) d -> p n d", p=128), "na"))
            psO, psD = self.HB.next(), self.HB.next()
            for ti, (p, cfg) in enumerate(lst):
                ps = self.FB.next()
                for h in range(2):
                    hs = slice(h * 64, (h + 1) * 64)
                    O.mm(ps[:, h * 256:(h + 1) * 256], KT[hs, ti * 128:(ti + 1) * 128], QT[hs, 0:256], inc=(h == 1))
                tmp, PT = N["tmp"].next(), N["PT"].next()
                O.stt("dve", tmp[:, :], ps[:, :], 0.125, self.nab[:, cfg * 512:(cfg + 1) * 512], mult, add)
                O.act(PT[:, :], tmp[:, :], AF.Exp)
                for h in range(2):
                    hs = slice(h * 64, (h + 1) * 64)
                    O.mm(psO[hs, 0:256], Vt[:, ti * 128 + h * 64:ti * 128 + (h + 1) * 64], PT[:, h * 256:(h + 1) * 256],
                         start=(ti == 0), stop=False, inc=False)
                    O.mm(psD[hs, 0:256], self.onesB[:, 0:64], PT[:, h * 256:(h + 1) * 256], start=(ti == 0), stop=False, inc=False)
            psm = self.FB.next()
            for h in range(2):
                hs = slice(h * 64, (h + 1) * 64)
                O.mm(psm[0:16, h * 256:(h + 1) * 256], N["KmT"][hs, 0:16], QT[hs, 0:256], inc=(h == 1))
            O.act(N["PmT"][0:16, :], psm[0:16, :], AF.Exp, scale=0.125)
            for h in range(2):
                hs = slice(h * 64, (h + 1) * 64)
                O.mm(psO[hs, 0:256], N["Vm"][0:16, hs], N["PmT"][0:16, h * 256:(h + 1) * 256], start=False, stop=True, inc=False)
                O.mm(psD[hs, 0:256], self.onesB[0:16, 0:64], N["PmT"][0:16, h * 256:(h + 1) * 256], start=False, stop=True, inc=(h == 1))
            O.recip(N["rec"][:, :], psD[:, 0:256])
            O.tt("dve", N["of"][:, :], psO[:, 0:256], N["rec"][:, :], mult)
            ob = N["ob"].next()
            O.tt("pool", ob[:, :], N["of"][:, :], sg[:, :], mult)
            for ci in range(2):
                for dv in odst(0, c0 + ci):
                    O.dma(dv, ob[:, ci * 128:(ci + 1) * 128])

    def alloc_O(self):
        sb = self.sb
        Q = {}
        Q["wob"] = sb("wob", [128, 12 * 1024], BF16)
        Q["fg"] = sb("fgbc", [128, 1024])
        Q["oT"] = RR([sb("oT%d" % i, [128, 12 * 128], BF16) for i in range(2)])
        Q["hx"] = RR([sb("ohx%d" % i, [128, 1024]) for i in range(2)])
        Q["hn"] = RR([sb("ohn%d" % i, [128, 1024]) for i in range(2)])
        Q["ot"] = RR([sb("oot%d" % i, [128, 1024]) for i in range(2)])
        Q["junk"] = sb("ojunk", [128, 1024], BF16)
        Q["ss"] = RR([sb("oss%d" % i, [128, 4]) for i in range(2)])
        self.Q = Q

    def o_phase(self, l, wo, fg, osrc, hres, hdst, outdst):
        O, Q = self.O, self.Q
        mult, add = ALU.mult, ALU.add
        final = outdst is not None
        for kc in range(12):
            self.load_cast(Q["wob"], kc * 1024, wo, lambda c, w, kc=kc: wo.ap[kc * 128:(kc + 1) * 128, c:c + w], 1024, 0)
        if final:
            O.dma(Q["fg"][:, :], fg.v(fg.ap[:, :], 0))
        for m in range(self.ntl):
            oT = Q["oT"].next()
            for i in range(4):
                O.dma(oT.v3(i * 384, 3, 128, 0, 128), osrc(m, i))
            hx = Q["hx"].next()
            O.dma(hx[:, :], hres(m))
            hn = Q["hn"].next()
            for nn in range(2):
                ps = self.FB.next()
                cs = slice(nn * 512, (nn + 1) * 512)
                for kc in range(12):
                    O.mm(ps[:, :], oT[:, kc * 128:(kc + 1) * 128], Q["wob"][:, kc * 1024 + nn * 512:kc * 1024 + (nn + 1) * 512],
                         start=(kc == 0), stop=(kc == 11), inc=(kc == 11))
                if m == 0:
                    O.stt("dve", hn[:, cs], ps[:, :], self.pvc("padrow"), hx[:, cs], mult, add)
                else:
                    O.tt("dve", hn[:, cs], ps[:, :], hx[:, cs], add)
            if not final:
                O.dma(hdst(m), hn[:, :])
            elif m >= 1:
                ss = Q["ss"].next()
                O.act(Q["junk"][:, :], hn[:, :], AF.Square, accum=ss[:, 0:1])
                O.act(ss[:, 1:2], ss[:, 0:1], AF.Sqrt, bias=1e-6, scale=1.0 / DM)
                O.recip(ss[:, 2:3], ss[:, 1:2])
                ot = Q["ot"].next()
                O.stt("dve", ot[:, :], hn[:, :], ss[:, 2:3], Q["fg"][:, :], mult, mult)
                O.dma(outdst(m), ot[:, :])


def build(nreal, phases):
    C = Ctx(nreal)
    has = set(phases)
    ntl, npq = C.ntl, C.npq
    C.setup()
    C.setup_M()
    EI, EO = "ExternalInput", "ExternalOutput"
    D = {}
    if "M0" in has:
        D["hg0"] = C.dram("hg0", [4 * ntl * 128, 1024], F32, EI)
    if "M1" in has:
        D["hg1"] = C.dram("hg1", [4 * ntl * 128, 1024], F32, "Internal" if "O0" in has else EI)
    for l in range(2):
        m, o = "M%d" % l in has, "O%d" % l in has
        if m:
            D["ogi%d" % l] = C.dram("ogi%d" % l, [4 * 384, ntl * 128], BF16, "Internal" if o else EO)
        if o:
            D["ogo%d" % l] = C.dram("ogo%d" % l, [4 * 384, ntl * 128], BF16, "Internal" if m else EI)
            D["wo%d" % l] = C.dram("wo%d" % l, [1536, 1024], F32, EI)
    if "O0" in has:
        D["hmine"] = C.dram("hmine", [ntl * 128, 1024], F32, EI)
        D["hgi"] = C.dram("hgi", [ntl * 128, 1024], F32, "Internal" if "M1" in has else EO)
    elif "O1" in has:
        D["hgi"] = C.dram("hgi", [ntl * 128, 1024], F32, EI)
    if "O1" in has:
        D["fg"] = C.dram("fg", [128, 1024], F32, EI)
        D["out"] = C.dram("out", [npq * 128, 1024], F32, EO)
    if "M0" in has and "M1" not in has:
        C.S["vfirst"] = C.dram("vfirst_o", [128, C.TP], F32, EO)
    if "M1" in has and "M0" not in has:
        C.S["vfirst"] = C.dram("vfirst_i", [128, C.TP], F32, EI)
    C.arena_start()
    groups = [[0, 1, 2, 3], [4, 5, 6, 7]]

    def hsrc(l):
        t = D["hg%d" % l]

        def f(n):
            if n == 0:
                r0 = 0
            else:
                r0 = ((n - 1) // npq) * ntl * 128 + (1 + (n - 1) % npq) * 128
            return t.v(t.ap[r0:r0 + 128, :], ("h", n))
        return f

    def odst(l):
        t = D["ogi%d" % l]

        def f(blk, n):
            if n == 0:
                return [t.v(t.ap[dq * 384 + blk * 128:dq * 384 + (blk + 1) * 128, 0:128], (blk, n, dq)) for dq in range(4)]
            dq, loc = (n - 1) // npq, 1 + (n - 1) % npq
            return [t.v(t.ap[dq * 384 + blk * 128:dq * 384 + (blk + 1) * 128, loc * 128:(loc + 1) * 128], (blk, n, dq))]
        return f

    def osrc(l):
        t = D["ogo%d" % l]
        return lambda m, i: t.v(t.ap[i * 384:(i + 1) * 384, m * 128:(m + 1) * 128].rearrange("(b p) t -> p b t", p=128), ("o", m, i))

    for ph in phases:
        l = int(ph[1])
        if ph[0] == "M":
            C.arena_reset()
            C.load_M_params(l)
            C.alloc_sweep1()
            C.sweep1(l, hsrc(l))
            C.arena_reset()
            if KSTOP in ("", "N", "S"):
                if KSTOP != "S":
                    C.alloc_NA()
                    C.na_phase(odst(l))
                    C.arena_reset()
                if KSTOP != "N":
                    C.setup_sweep2()
                    C.sweep2(l, odst(l))
                    C.arena_reset()
            if "O%d" % l in has:
                gi, go = D["ogi%d" % l], D["ogo%d" % l]
                C.P.collective("AllToAll", groups, gi.v(gi.ap[:, :], "cc"), go.v(go.ap[:, :], "cc"))
                C.P.barrier()
        else:
            C.arena_reset()
            C.alloc_O()
            if l == 0:
                C.O.dma(C.pv[:, :], C.Min[0]["pv"].v(C.Min[0]["pv"].ap[:, :], 1))
            hg, hm = D["hgi"], D.get("hmine")
            hres = (lambda m: hm.v(hm.ap[m * 128:(m + 1) * 128, :], m)) if l == 0 else (lambda m: hg.v(hg.ap[m * 128:(m + 1) * 128, :], ("r", m)))
            hdst = (lambda m: hg.v(hg.ap[m * 128:(m + 1) * 128, :], ("w", m))) if l == 0 else None
            outdst = None
            if l == 1:
                ot = D["out"]
                outdst = lambda m: ot.v(ot.ap[(m - 1) * 128:m * 128, :], m)
            C.o_phase(l, D["wo%d" % l], D.get("fg"), osrc(l), hres, hdst, outdst)
            C.arena_reset()
            if l == 0 and "M1" in has:
                h1 = D["hg1"]
                C.P.collective("AllGather", groups, hg.v(hg.ap[:, :], "cc"), h1.v(h1.ap[:, :], "cc"))
                C.P.barrier()
    C.P.barrier()
    with C.nc.Block() as block:
        C.P.emit(block)
    C.es.close()
    return C


def host_layouts(inp, nreal):
    f = np.float32
    npq, ntl = nreal // 4, nreal // 4 + 1
    T = nreal * 128
    x, meta = inp["x"], inp["meta"]
    cm = const_masks()
    blocks, tiles = na_plan(nreal)
    per = []
    for c in range(8):
        b, q = c // 4, c % 4
        hg = np.zeros((4, ntl * 128, 1024), f)
        for i in range(4):
            hg[i, 112:128] = meta
            hg[i, 128:] = x[b, i * npq * 128:(i + 1) * npq * 128]
        d = {"cmask": cm, "hg0": hg.reshape(4 * ntl * 128, 1024), "hmine": np.ascontiguousarray(hg[q])}
        for l in range(2):
            cp = core_params(inp, l, q)
            for k, v in cp.items():
                d["%s%d" % (k, l)] = np.ascontiguousarray(v, dtype=f)
            d["nab%d" % l] = na_bias_tiles(inp["na_rpb"][l][2 * q:2 * q + 2], tiles)
            wo = inp["w_out"][l]
            d["wo%d" % l] = np.ascontiguousarray(np.concatenate(
                [wo[blk * 512 + 128 * i:blk * 512 + 128 * (i + 1)] for i in range(4) for blk in range(3)], axis=0))
        d["fg"] = np.ascontiguousarray(np.broadcast_to(inp["final_norm_g"][None, :], (128, 1024)), dtype=f)
        per.append(d)
    return per


_PROGS = {}


def get_prog(nreal, phases):
    key = (nreal, tuple(phases))
    if key not in _PROGS:
        _PROGS[key] = build(nreal, list(phases))
    return _PROGS[key]


M_IN = ["cmask", "w", "gbc", "pv", "wup", "aup", "gup", "vdn", "vup", "nab"]


def in_names(phases):
    has = set(phases)
    names = ["cmask"]
    for l in range(2):
        names += ["%s%d" % (k, l) for k in M_IN[1:]]
    if "M0" in has:
        names.append("hg0")
    for l in range(2):
        if "O%d" % l in has:
            names.append("wo%d" % l)
    if "O0" in has:
        names.append("hmine")
    if "O1" in has:
        names.append("fg")
    return names


FUSED = False
NREAL = 128


def kernel(**inp):
    inp = {k: np.asarray(v) for k, v in inp.items()}
    nreal = NREAL
    npq, ntl = nreal // 4, nreal // 4 + 1
    per = host_layouts(inp, nreal)
    B = inp["x"].shape[0]
    out = np.empty((B, nreal * 128, 1024), np.float32)
    cores = list(range(8))
    if FUSED:
        ph = ["M0", "O0", "M1", "O1"]
        C = get_prog(nreal, ph)
        maps = [{k: per[c][k] for k in in_names(ph)} for c in cores]
        res = run_bass_kernel_spmd(C.nc, maps, core_ids=cores).results
    else:
        def run(ph, extra):
            C = get_prog(nreal, ph)
            maps = []
            for c in cores:
                m = {k: per[c][k] for k in in_names(ph)}
                m.update(extra[c])
                maps.append(m)
            return run_bass_kernel_spmd(C.nc, maps, core_ids=cores).results

        def a2a(r, name):
            og = [np.asarray(r[c][name]).reshape(4, 384, ntl * 128) for c in cores]
            return [np.ascontiguousarray(np.stack([og[4 * (c // 4) + i][c % 4] for i in range(4)])).reshape(4 * 384, ntl * 128) for c in cores]

        r = run(["M0"], [{} for _ in cores])
        vf = [np.asarray(r[c]["vfirst_o"]) for c in cores]
        ogo = a2a(r, "ogi0")
        r = run(["O0"], [{"ogo0": ogo[c]} for c in cores])
        hgi = [np.asarray(r[c]["hgi"]) for c in cores]
        hg1 = [np.ascontiguousarray(np.concatenate([hgi[4 * (c // 4) + i] for i in range(4)], axis=0)) for c in cores]
        r = run(["M1"], [{"hg1": hg1[c], "vfirst_i": vf[c]} for c in cores])
        ogo = a2a(r, "ogi1")
        res = run(["O1"], [{"ogo1": ogo[c], "hgi": hgi[c]} for c in cores])
    for c in cores:
        b, q = c // 4, c % 4
        out[b, q * npq * 128:(q + 1) * npq * 128] = np.asarray(res[c]["out"])
    return out
```

```python
import numpy as np
from contextlib import ExitStack
import concourse.bass as bass
import concourse.mybir as mybir
from concourse.bass_utils import run_bass_kernel_spmd

F32 = mybir.dt.float32
BF16 = mybir.dt.bfloat16
AF = mybir.ActivationFunctionType
ALU = mybir.AluOpType

DM = 1024
KC = 8
SEM_LIMIT = 28000

CM_MF2, CM_MB2, CM_SL2, CM_SU2, CM_GMIX, CM_GFF, CM_ID, CM_BO, CM_PAD, CM_ONE = 0, 512, 1024, 1280, 1536, 1792, 2048, 2176, 2304, 2432
CM_W = 2560


def const_masks():
    i = np.arange(128)
    SU = (i[:, None] < i[None, :]).astype(np.float32)
    IU = (i[:, None] <= i[None, :]).astype(np.float32)
    SL = (i[:, None] > i[None, :]).astype(np.float32)
    IL = (i[:, None] >= i[None, :]).astype(np.float32)
    ID = np.eye(128, dtype=np.float32)
    BO = np.zeros((128, 128), np.float32)
    BO[:64, :64] = 1
    BO[64:, 64:] = 1
    PAD = np.zeros((128, 128), np.float32)
    PAD[:, 112:] = 1
    ONE = np.ones((128, 128), np.float32)
    return np.concatenate([SU, IU, SU, IU, SL, IL, SL, IL, SL, SL, SU, SU, IU, IL, IU, IU, ID, BO, PAD, ONE], axis=1)


class Buf:
    __slots__ = ("last_w", "readers")

    def __init__(self):
        self.last_w = None
        self.readers = {}


class V:
    __slots__ = ("ap", "b")

    def __init__(self, ap, b):
        self.ap = ap
        self.b = b


class TT:
    def __init__(self, h, b=None):
        self.h = h
        self.b = b if b is not None else Buf()

    def __getitem__(self, idx):
        return V(self.h[idx], self.b)

    def v3(self, base, nblk, stride, lo, hi, p0=None, p1=None):
        full = self.h[:, :] if p0 is None else self.h[p0:p1, :]
        Wd = full.shape[-1]
        assert Wd % stride == 0 and (base % stride) + hi <= stride, (Wd, base, stride, hi)
        a0, r = base // stride, base % stride
        ap = full.rearrange("p (a b) -> p a b", b=stride)[:, a0:a0 + nblk, r + lo:r + hi]
        return V(ap, self.b)

    def v3p(self, p0, p1, base, nblk, stride, lo, hi):
        return self.v3(base, nblk, stride, lo, hi, p0, p1)


class DT:
    def __init__(self, ap):
        self.ap = ap
        self.bufs = {}

    def v(self, ap, key):
        b = self.bufs.get(key)
        if b is None:
            b = self.bufs[key] = Buf()
        return V(ap, b)


class SemSlot:
    def __init__(self, prog, name, ekey):
        self.prog, self.name, self.ekey = prog, name, ekey
        self.sem = None
        self.cnt = 0
        self.n = 0

    def _roll(self, inc):
        if self.sem is None or self.cnt + inc > SEM_LIMIT:
            self.sem = self.prog.new_sem(f"{self.name}_{self.n}")
            self.n += 1
            self.cnt = 0

    def bump(self, inc):
        self._roll(inc)
        self.cnt += inc
        return (self.sem, self.cnt, self.ekey)

    def peek(self, inc):
        self._roll(inc)
        return (self.sem, self.cnt + inc, self.ekey)

    def cur(self):
        return (self.sem, self.cnt, self.ekey)


class Prog:
    ENG = ("pe", "dve", "act", "pool", "sp")

    def __init__(self, nc, es, ndma=12):
        self.nc, self.es = nc, es
        self.streams = {e: [] for e in self.ENG}
        self.slot = {e: SemSlot(self, "s" + e, e) for e in self.ENG}
        self.seen = {e: {} for e in self.ENG}
        self.dslots = [SemSlot(self, f"d{i}", "dma") for i in range(ndma)]
        self.drr = 0
        self.ninstr = {e: 0 for e in self.ENG}
        self.nsem = 0
        self.pe_serial = False

    def new_sem(self, name):
        self.nsem += 1
        return self.es.enter_context(self.nc.semaphore(name))

    def _wait(self, e, tok):
        sem, val, _ = tok
        k = id(sem)
        if self.seen[e].get(k, 0) >= val:
            return
        self.seen[e][k] = val
        self.streams[e].append(lambda eng, sem=sem, val=val: eng.wait_ge(sem, val))

    def _deps(self, e, reads, writes):
        for b in reads:
            if b.last_w is not None:
                self._wait(e, b.last_w)
        for b in writes:
            if b.last_w is not None and not (e == "pe" and b.last_w[2] == e):
                self._wait(e, b.last_w)
            for r in b.readers.values():
                if not (e == "pe" and r[2] == e):
                    self._wait(e, r)

    def _mark(self, tok, reads, writes):
        for b in reads:
            b.readers[(id(tok[0]), tok[2])] = tok
        for b in writes:
            b.last_w = tok
            b.readers = {}

    def op(self, e, fn, reads=(), writes=(), inc=True, serial=False):
        if e == "pe":
            inc = True
            if (serial or self.pe_serial) and self.slot["pe"].sem is not None and self.slot["pe"].cnt > 0:
                self._wait("pe", self.slot["pe"].cur())
            self.pe_serial = serial
        self._deps(e, reads, writes)
        self.ninstr[e] += 1
        if inc:
            tok = self.slot[e].bump(1)
            sem = tok[0]
            self.streams[e].append(lambda eng, fn=fn, sem=sem: fn(eng).then_inc(sem, 1))
        else:
            tok = self.slot[e].peek(1)
            self.streams[e].append(lambda eng, fn=fn: fn(eng))
        self._mark(tok, reads, writes)
        return tok

    def dma(self, out, in_, q="sp", **kw):
        reads, writes = [in_.b], [out.b]
        self._deps(q, reads, writes)
        s = self.dslots[self.drr]
        self.drr = (self.drr + 1) % len(self.dslots)
        if s.sem is not None and s.cnt > 0:
            self._wait(q, s.cur())
        tok = s.bump(16)
        sem = tok[0]
        self.ninstr[q] += 1
        oa, ia = out.ap, in_.ap
        self.streams[q].append(lambda eng, oa=oa, ia=ia, sem=sem, kw=kw: eng.dma_start(out=oa, in_=ia, **kw).then_inc(sem, 16))
        self._mark(tok, reads, writes)
        return tok

    def collective(self, kind, groups, in_v, out_v):
        q = "pool"
        reads, writes = [in_v.b], [out_v.b]
        self._deps(q, reads, writes)
        s = self.dslots[self.drr]
        self.drr = (self.drr + 1) % len(self.dslots)
        if s.sem is not None and s.cnt > 0:
            self._wait(q, s.cur())
        tok = s.bump(16)
        sem = tok[0]
        ia, oa = in_v.ap, out_v.ap
        self.streams[q].append(lambda eng: eng.collective_compute(kind, ALU.bypass, replica_groups=groups, ins=[ia], outs=[oa]).then_inc(sem, 16))
        self._mark(tok, reads, writes)

    def barrier(self):
        toks = [self.slot[e].cur() for e in self.ENG if self.slot[e].sem is not None and self.slot[e].cnt > 0]
        toks += [s.cur() for s in self.dslots if s.sem is not None and s.cnt > 0]
        for e in self.ENG:
            for t in toks:
                self._wait(e, t)

    def finish(self, bufs, q="sp"):
        for b in bufs:
            if b.last_w is not None:
                self._wait(q, b.last_w)

    def emit(self, block):
        st = self.streams

        @block.tensor
        def _(eng):
            for f in st["pe"]:
                f(eng)

        @block.vector
        def _(eng):
            for f in st["dve"]:
                f(eng)

        @block.scalar
        def _(eng):
            for f in st["act"]:
                f(eng)

        @block.gpsimd
        def _(eng):
            for f in st["pool"]:
                f(eng)

        @block.sync
        def _(eng):
            for f in st["sp"]:
                f(eng)


def _rw(ins, outs):
    r, w = [], []
    for x in ins:
        if isinstance(x, V) and x.b not in r:
            r.append(x.b)
    for x in outs:
        if isinstance(x, V) and x.b not in w:
            w.append(x.b)
    return r, w


def _a(x):
    return x.ap if isinstance(x, V) else x


class Ops:
    def __init__(self, P):
        self.P = P

    def act(self, out, in_, func, bias=None, scale=None, accum=None):
        r, w = _rw([in_, bias, scale], [out, accum])
        kw = {}
        if bias is not None:
            kw["bias"] = _a(bias)
        if scale is not None:
            kw["scale"] = _a(scale)
        if accum is not None:
            kw["accum_out"] = _a(accum)
        oa, ia = out.ap, in_.ap
        self.P.op("act", lambda e: e.activation(out=oa, in_=ia, func=func, **kw), r, w)

    def copy(self, eng, out, in_):
        r, w = _rw([in_], [out])
        oa, ia = out.ap, in_.ap
        if eng == "act":
            self.P.op("act", lambda e: e.activation(out=oa, in_=ia, func=AF.Copy), r, w)
        else:
            self.P.op(eng, lambda e: e.tensor_copy(out=oa, in_=ia), r, w)

    def memset(self, eng, out, val):
        r, w = _rw([], [out])
        oa = out.ap
        self.P.op(eng, lambda e: e.memset(oa, val), r, w)

    def tt(self, eng, out, in0, in1, op):
        r, w = _rw([in0, in1], [out])
        oa, a0, a1 = out.ap, in0.ap, in1.ap
        self.P.op(eng, lambda e: e.tensor_tensor(out=oa, in0=a0, in1=a1, op=op), r, w)

    def ts(self, eng, out, in0, s1, op0, s2=None, op1=None):
        r, w = _rw([in0, s1, s2], [out])
        oa, a0, x1, x2 = out.ap, in0.ap, _a(s1), _a(s2)
        if op1 is None:
            self.P.op(eng, lambda e: e.tensor_scalar(out=oa, in0=a0, scalar1=x1, scalar2=None, op0=op0), r, w)
        else:
            self.P.op(eng, lambda e: e.tensor_scalar(out=oa, in0=a0, scalar1=x1, scalar2=x2, op0=op0, op1=op1), r, w)

    def stt(self, eng, out, in0, scalar, in1, op0, op1):
        r, w = _rw([in0, scalar, in1], [out])
        oa, a0, sc, a1 = out.ap, in0.ap, _a(scalar), in1.ap
        self.P.op(eng, lambda e: e.scalar_tensor_tensor(out=oa, in0=a0, scalar=sc, in1=a1, op0=op0, op1=op1), r, w)

    def scan(self, out, d0, d1):
        r, w = _rw([d0, d1], [out])
        oa, a0, a1 = out.ap, d0.ap, d1.ap
        self.P.op("dve", lambda e: e.tensor_tensor_scan(out=oa, data0=a0, data1=a1, initial=0.0, op0=ALU.mult, op1=ALU.add), r, w)

    def recip(self, out, in_):
        r, w = _rw([in_], [out])
        oa, ia = out.ap, in_.ap
        self.P.op("dve", lambda e: e.reciprocal(out=oa, in_=ia), r, w)

    def mm(self, out, lhsT, rhs, start=True, stop=True, inc=True):
        r, w = _rw([lhsT, rhs], [out])
        oa, la, ra = out.ap, lhsT.ap, rhs.ap
        serial = la.shape[0] < 128
        self.P.op("pe", lambda e: e.matmul(oa, lhsT=la, rhs=ra, start=start, stop=stop), r, w, inc=inc, serial=serial)

    def tr(self, out, in_, ident, inc=True):
        r, w = _rw([in_, ident], [out])
        oa, ia, da = out.ap, in_.ap, ident.ap
        self.P.op("pe", lambda e: e.transpose(out=oa, in_=ia, identity=da), r, w, inc=inc)

    def dma(self, out, in_, **kw):
        self.P.dma(out, in_, **kw)


class RR:
    def __init__(self, items):
        self.items = items
        self.i = 0

    def next(self):
        x = self.items[self.i]
        self.i = (self.i + 1) % len(self.items)
        return x


NAQ, NAK, NAG, GQ, GK, GG, GD, RR_, RK, RV, RWD, RAD, RG = range(13)
NFM = (13, 16)
SHIFT_CTS = ((RR_, RK, RV, RWD, RAD), (RR_, RK, RV, RWD, RAD, 13, 14, 15))
NCW_MAX = 16 * 128 + 256
PVN = ["cw%d_%d" % (j, t) for j in range(8) for t in range(3)] + [
    "w0_0", "w0_1", "a0_0", "a0_1", "k_k", "k_a", "r_k", "ln_g", "ln_b", "v0", "g_b", "gng", "negb", "padrow", "omka"]
PV = {n: i for i, n in enumerate(PVN)}
NPV = len(PVN)

NA_B, GLA_B, RW_B = 0, 2048, 2048 + 1568


def core_cols(l, q):
    a = np.arange
    fm = [NA_B + 128 * q + a(128), NA_B + 512 + 128 * q + a(128), NA_B + 1536 + 128 * q + a(128),
          GLA_B + 64 * q + np.concatenate([a(64), a(64)]), GLA_B + 256 + 64 * q + np.concatenate([a(64), a(64)]),
          GLA_B + 1024 + 128 * q + a(128), GLA_B + 1536 + np.tile(a(32), 4),
          RW_B + 128 * q + a(128), RW_B + 512 + 128 * q + a(128), RW_B + 1024 + 128 * q + a(128),
          RW_B + 1536 + a(128), RW_B + 1664 + a(128), RW_B + 1792 + 128 * q + a(128)]
    others = [o for o in range(4) if o != q]
    if l == 1:
        fm += [RW_B + 1024 + 128 * o + a(128) for o in others]
    tm = [NA_B + 1024 + 128 * q + a(128), GLA_B + 512 + 128 * q + a(128)]
    return np.concatenate(fm + tm)


def core_params(inp, l, q):
    f = np.float32
    others = [o for o in range(4) if o != q]
    pv = np.zeros((128, NPV), f)
    conv = inp["rw_conv"][l]
    shift_off = [128 * q, 512 + 128 * q, 1024 + 128 * q, 1536, 1664] + [1024 + 128 * o for o in others]
    for j, off in enumerate(shift_off):
        for t in range(3):
            pv[:, PV["cw%d_%d" % (j, t)]] = conv[t, off:off + 128]
    sl = slice(128 * q, 128 * q + 128)
    for d in range(2):
        pv[:, PV["w0_%d" % d]] = inp["rw_w0"][l, d, sl]
        pv[:, PV["a0_%d" % d]] = inp["rw_a0"][l, d, sl]
    for n, k in (("k_k", "rw_k_k"), ("k_a", "rw_k_a"), ("r_k", "rw_r_k"), ("ln_g", "rw_ln_g"), ("ln_b", "rw_ln_b")):
        pv[:, PV[n]] = inp[k][l, sl]
    if l == 1:
        pv[:, PV["v0"]] = inp["rw_v0"][0, sl]
    pv[:64, PV["g_b"]] = inp["gla_g_b"][l, 0, 64 * q:64 * q + 64]
    pv[64:, PV["g_b"]] = inp["gla_g_b"][l, 1, 64 * q:64 * q + 64]
    pv[:, PV["gng"]] = inp["gla_norm_g"][l]
    pv[112:, PV["padrow"]] = 1.0
    out = {"pv": pv}
    out["w"] = np.ascontiguousarray(inp["w_in"][l][:, core_cols(l, q)])
    if l == 0:
        out["w"] = np.concatenate([out["w"][:, :13 * 128], np.zeros((1024, 3 * 128), f), out["w"][:, 13 * 128:]], axis=1)
    out["gbc"] = np.ascontiguousarray(np.broadcast_to(inp["norm_g"][l][None, :], (128, 1024)))
    out["wup"] = np.concatenate([inp["rw_w_up"][l, 0][:, sl], inp["rw_w_up"][l, 1][:, sl]], axis=0)
    out["aup"] = np.concatenate([inp["rw_a_up"][l, 0][:, sl], inp["rw_a_up"][l, 1][:, sl]], axis=0)
    gup = np.zeros((32, 128), f)
    gup[:16, :64] = inp["gla_g_up"][l, 0][:, 64 * q:64 * q + 64]
    gup[16:, 64:] = inp["gla_g_up"][l, 1][:, 64 * q:64 * q + 64]
    out["gup"] = gup
    if l == 1:
        vd = inp["rw_v_down"][0]
        out["vdn"] = np.ascontiguousarray(np.stack([vd[128 * o:128 * o + 128] for o in [q] + others], axis=1))
        out["vup"] = np.ascontiguousarray(inp["rw_v_up"][0][:, sl])
    else:
        out["vdn"] = np.zeros((128, 4, 32), f)
        out["vup"] = np.zeros((32, 128), f)
    out["vdn"] = out["vdn"].reshape(128, 128)
    return out


def na_plan(nreal):
    rows = nreal * 2
    wr = min(8, rows)
    cols = np.arange(64)
    cstart = np.clip(cols - 8, 0, 48)
    cfgs = {}
    tiles = []
    blocks = []
    for B in range(nreal // 2):
        lst = []
        for p in range(rows // 2):
            idx = np.full((128, 256), -1, np.int64)
            anyv = False
            for qr in range(4):
                r = 4 * B + qr
                rs = int(np.clip(r - wr // 2, 0, rows - wr))
                for kr2 in range(2):
                    kr = 2 * p + kr2
                    if not (rs <= kr < rs + wr):
                        continue
                    dr = kr - r + 7
                    for qc in range(64):
                        kcs = cstart[qc] + np.arange(16)
                        idx[kr2 * 64 + kcs, qr * 64 + qc] = dr * 31 + (kcs - qc + 15)
                    anyv = True
            if not anyv:
                continue
            key = idx.tobytes()
            if key not in cfgs:
                cfgs[key] = len(tiles)
                tiles.append(idx)
            lst.append((p, cfgs[key]))
        blocks.append(lst)
    return blocks, tiles


def na_bias_tiles(rpb2, tiles):
    out = np.empty((128, len(tiles), 2, 256), np.float32)
    for c, idx in enumerate(tiles):
        for h in range(2):
            flat = rpb2[h].reshape(-1)
            out[:, c, h, :] = np.where(idx >= 0, flat[np.maximum(idx, 0)], np.float32(-30000.0))
    return out.reshape(128, -1)


TMOFF = 16 * 128
NT = 256
import os
KSTOP = os.environ.get("KSTOP", "")
EM05 = float(np.exp(-0.5))


class Ctx:
    def __init__(self, nreal):
        self.nreal = nreal
        self.nchk = nreal + 1
        self.TP = self.nchk * 128
        self.npq = nreal // 4
        self.ntl = self.npq + 1
        self.nc = bass.Bass("TRN2", target_bir_lowering=False)
        self.es = ExitStack()
        self.P = Prog(self.nc, self.es)
        self.O = Ops(self.P)
        self.blocks, self.na_tiles = na_plan(nreal)
        self.ncfg = len(self.na_tiles)
        self.ext_out = []
        self.arena = None

    def sb(self, name, shape, dt=F32):
        if self.arena is None:
            return TT(self.es.enter_context(self.nc.sbuf_tensor(name, list(shape), dt)))
        k = 1 if dt == BF16 else 0
        ncol = shape[1] + (shape[1] % 2)
        off = self.arena_off[k]
        self.arena_off[k] += ncol
        assert self.arena_off[k] <= self.arena_cols[k], (name, k, self.arena_off[k], self.arena_cols[k])
        return TT(self.arena[k][0:shape[0], off:off + shape[1]])

    def arena_start(self, cols32=13000, cols16=23000):
        self.arena = [self.es.enter_context(self.nc.sbuf_tensor("arena32", [128, cols32], F32)),
                      self.es.enter_context(self.nc.sbuf_tensor("arena16", [128, cols16], BF16))]
        self.arena_cols = [cols32, cols16]
        self.arena_off = [0, 0]
        self.arena_hw = [0, 0]

    def arena_reset(self):
        self.arena_hw = [max(a, b) for a, b in zip(self.arena_hw, self.arena_off)]
        self.arena_off = [0, 0]
        self.P.barrier()

    def ps(self, name, shape, dt=F32):
        return TT(self.es.enter_context(self.nc.psum_tensor(name, list(shape), dt)))

    def dram(self, name, shape, dt, kind="Internal"):
        if kind == "Internal":
            t = self.nc.dram_tensor(name, list(shape), dt)
        else:
            t = self.nc.dram_tensor(name, list(shape), dt, kind=kind)
        return DT(t.ap())

    def setup(self):
        sb, ps = self.sb, self.ps
        self.T0 = ps("T0", [128, 1024], BF16)
        self.T1 = ps("T1", [128, 1024], BF16)
        self.FB = RR([ps("FB%d" % i, [128, 512]) for i in range(3)])
        hb0, hb1 = [], []
        for i in range(3):
            bank = self.es.enter_context(self.nc.psum_tensor("HBK%d" % i, [128, 512], F32))
            bb = Buf()
            hb0.append(TT(bank[:, 0:256], bb))
            hb1.append(TT(bank[:, 256:512], bb))
        self.HB = RR(hb0 + hb1)
        self.cm = sb("cm", [128, CM_W])
        self.identB = sb("identB", [128, 128], BF16)
        self.onesB = sb("onesB", [128, 64], BF16)
        self.zeroB = sb("zeroB", [128, 128], BF16)
        self.cmD = self.dram("cmask", [128, CM_W], F32, "ExternalInput")
        O = self.O
        O.dma(self.cm[:, :], self.cmD.v(self.cmD.ap[:, :], 0))
        O.copy("dve", self.identB[:, :], self.cm[:, CM_ID:CM_ID + 128])
        O.copy("dve", self.onesB[:, :], self.cm[:, CM_ONE:CM_ONE + 64])
        O.memset("pool", self.zeroB[:, :], 0.0)
        self.wstage = RR([sb("wst%d" % i, [128, 1024]) for i in range(2)])

    def cmv(self, off, w):
        return self.cm[:, off:off + w]

    def load_cast(self, dst_tt, dcol0, src_dt, src_ap_fn, ncols, key, rows=128):
        c = 0
        while c < ncols:
            w = min(1024, ncols - c)
            st = self.wstage.next()
            self.O.dma(st[0:rows, 0:w], src_dt.v(src_ap_fn(c, w), key))
            self.O.copy("pool", dst_tt[0:rows, dcol0 + c:dcol0 + c + w], st[0:rows, 0:w])
            c += w

    def setup_M(self):
        sb = self.sb
        self.Min = []
        for l in range(2):
            d = {}
            for n, shp in (("w", [1024, NCW_MAX]), ("gbc", [128, 1024]), ("pv", [128, NPV]), ("wup", [128, 128]),
                           ("aup", [128, 128]), ("gup", [32, 128]), ("vdn", [128, 128]), ("vup", [32, 128]),
                           ("nab", [128, self.ncfg * 512])):
                d[n] = self.dram("%s%d" % (n, l), shp, F32, "ExternalInput")
            self.Min.append(d)
        self.wb = sb("wb", [128, KC * NCW_MAX], BF16)
        self.gbc = sb("gbc", [128, 1024])
        self.pv = sb("pv", [128, NPV])
        self.wupB = sb("wupB", [128, 128], BF16)
        self.aupB = sb("aupB", [128, 128], BF16)
        self.gupF = sb("gupF", [32, 128])
        self.vdnB = sb("vdnB", [128, 128], BF16)
        self.vupB = sb("vupB", [32, 128], BF16)
        self.nab = sb("nab", [128, self.ncfg * 512])
        TP = self.TP
        dr = self.dram
        self.S = {n: dr(n, shp, dt) for n, shp, dt in (
            ("naq", [128, TP], BF16), ("nak", [128, TP], BF16), ("nag", [128, TP], BF16), ("nav", [TP, 128], BF16),
            ("rY1", [128, TP], F32), ("rRT", [128, TP], BF16), ("rPD", [self.nchk, 128, 128], F32),
            ("rBV", [128, TP], F32), ("rSG", [128, TP], BF16),
            ("gY1", [128, TP], F32), ("gQT", [128, TP], BF16), ("gD", [self.nchk, 128, 132], F32), ("gSG", [128, TP], BF16),
            ("vfirst", [128, TP], F32))}
        self.Wp = {}
        for n, shp, dt in (("Sg", [128, 128], F32), ("Sgb", [128, 128], BF16), ("Sf", [128, 64], F32), ("Sfb", [128, 64], BF16),
                           ("Sb", [128, 64], F32), ("Sbb", [128, 64], BF16)):
            self.Wp[n] = sb(n, shp, dt)

    def alloc_sweep1(self):
        sb = self.sb
        self.hx = RR([sb("hx%d" % i, [128, 1024]) for i in range(2)])
        self.junk = sb("junk", [128, 1024], BF16)
        self.hnb = RR([sb("hnb%d" % i, [128, 1024], BF16) for i in range(2)])
        self.hnT = [sb("hnT%d" % i, [128, 2 * 1024], BF16) for i in range(2)]
        self.ssq = RR([sb("ssq%d" % i, [128, 4]) for i in range(2)])
        self.stb = RR([sb("stb%d" % i, [128, 256], BF16) for i in range(4)])
        self.stf = RR([sb("stf%d" % i, [128, 256]) for i in range(4)])
        W = {}
        for n in ("q2", "k2", "lwg", "cumg", "epg", "eng", "erg", "rT", "kT", "vT", "vf", "kr", "tmpa", "tmpb", "inv", "kkn",
                  "lw0", "lw1", "al0", "al1", "km0", "km1", "b0", "b1", "cum0", "cum1", "cx", "eex", "epos", "eneg", "erem"):
            W[n] = sb(n, [128, NT])
        W["gdF"] = sb("gdF", [32, NT])
        for n in ("QhT", "KhTg", "KtTg", "tw", "adb", "vTb", "vo0", "vo1", "vo2", "vdb",
                  "BhT0", "BhT1", "KhT0", "KhT1", "BtT0", "BtT1", "KtT0", "KtT1"):
            W[n] = sb(n, [128, NT], BF16)
        W["AR0"] = sb("AR0", [128, 2 * NT], BF16)
        W["AR1"] = sb("AR1", [128, 2 * NT], BF16)
        W["gv"] = sb("gv", [128, NT], BF16)
        W["vtm"] = RR([sb("vtm%d" % i, [128, 256], BF16) for i in range(2)])
        for n in ("totg", "etotg", "tot0", "tot1", "etot0", "etot1"):
            W[n] = sb(n, [128, 4])
        W["zext"] = RR([sb("zext%d" % i, [128, NT + 2]) for i in range(2)])
        W["prevcol"] = sb("prevcol", [128, 8])
        W["hnN"] = sb("hnN", [128, 8], BF16)
        W["nxt"] = sb("nxt", [128, 8])
        W["AT"] = sb("AT", [128, 256], BF16)
        W["ktm"] = sb("ktm", [128, 128], BF16)
        W["gst"] = RR([sb("gst%d" % i, [128, 132]) for i in range(2)])
        W["tk"] = RR([sb("tk%d" % i, [128, 896], BF16) for i in range(2)])
        W["SC1"] = [sb("SC1_%d" % i, [128, 512], BF16) for i in range(2)]
        W["SC2"] = [sb("SC2_%d" % i, [128, 512], BF16) for i in range(2)]
        W["RX"] = [RR([sb("RX%d_%d" % (d, i), [128, 512], BF16) for i in range(2)]) for d in range(2)]
        W["LTn"] = [RR([sb("LTn%d_%d" % (d, i), [128, 256], BF16) for i in range(2)]) for d in range(2)]
        W["RpT"] = [sb("RpT%d" % d, [128, 128], BF16) for d in range(2)]
        W["PD"] = [RR([sb("PD%d_%d" % (d, i), [128, 128]) for i in range(2)]) for d in range(2)]
        W.update(self.Wp)
        self.W = W

    def pvc(self, name):
        c = PV[name]
        return self.pv[:, c:c + 1]

    def load_M_params(self, l):
        O, d = self.O, self.Min[l]
        for kc in range(KC):
            self.load_cast(self.wb, kc * NCW_MAX, d["w"], lambda c, w, kc=kc: d["w"].ap[kc * 128:(kc + 1) * 128, c:c + w], NCW_MAX, 0)
        O.dma(self.gbc[:, :], d["gbc"].v(d["gbc"].ap[:, :], 0))
        O.dma(self.pv[:, :], d["pv"].v(d["pv"].ap[:, :], 0))
        O.dma(self.gupF[:, :], d["gup"].v(d["gup"].ap[:, :], 0))
        O.dma(self.nab[:, :], d["nab"].v(d["nab"].ap[:, :], 0))
        self.load_cast(self.wupB, 0, d["wup"], lambda c, w: d["wup"].ap[:, c:c + w], 128, 0)
        self.load_cast(self.aupB, 0, d["aup"], lambda c, w: d["aup"].ap[:, c:c + w], 128, 0)
        self.load_cast(self.vdnB, 0, d["vdn"], lambda c, w: d["vdn"].ap[:, c:c + w], 128, 0)
        self.load_cast(self.vupB, 0, d["vup"], lambda c, w: d["vup"].ap[:, c:c + w], 128, 0, rows=32)
        O.ts("dve", self.pvc("negb"), self.pvc("g_b"), -1.0, ALU.mult)
        O.ts("dve", self.pvc("omka"), self.pvc("k_a"), -1.0, ALU.mult, 1.0, ALU.add)

    def st_chunks(self, i):
        if i == 0:
            return [0]
        return [2 * i - 1, 2 * i]

    def stage_A(self, i, hsrc):
        O = self.O
        hT = self.hnT[i % 2]
        for ci, n in enumerate(self.st_chunks(i)):
            hx = self.hx.next()
            O.dma(hx[:, :], hsrc(n))
            ss = self.ssq.next()
            O.act(self.junk[:, :], hx[:, :], AF.Square, accum=ss[:, 0:1])
            O.act(ss[:, 1:2], ss[:, 0:1], AF.Sqrt, bias=1e-6, scale=1.0 / DM)
            O.recip(ss[:, 2:3], ss[:, 1:2])
            hn = self.hnb.next()
            O.stt("dve", hn[:, :], hx[:, :], ss[:, 2:3], self.gbc[:, :], ALU.mult, ALU.mult)
            for kc in range(KC):
                O.tr(self.T0[:, kc * 128:(kc + 1) * 128], hn[:, kc * 128:(kc + 1) * 128], self.identB[:, :], inc=(kc == KC - 1))
            O.copy("act", hT[:, ci * 1024:(ci + 1) * 1024], self.T0[:, :])

    def rhs_h(self, i, kc, nch):
        return self.hnT[i % 2].v3(kc * 128, nch, 1024, 0, 128)

    def inproj_fm(self, i, ct, nch, M=128):
        O = self.O
        ps = self.HB.next()
        for kc in range(KC):
            O.mm(ps.v3p(0, M, 0, nch, 128, 0, 128), self.wb[:, kc * NCW_MAX + ct * 128:kc * NCW_MAX + ct * 128 + M],
                 self.rhs_h(i, kc, nch), start=(kc == 0), stop=(kc == KC - 1), inc=(kc == KC - 1))
        return ps

    def sweep1(self, l, hsrc):
        O, W, S = self.O, self.W, self.S
        cmv, pvc = self.cmv, self.pvc
        nst = self.nreal // 2 + 1
        mult, add, sub = ALU.mult, ALU.add, ALU.subtract
        self.stage_A(0, hsrc)
        O.memset("pool", W["prevcol"][:, :], 0.0)
        O.memset("pool", W["Sg"][:, :], 0.0)
        O.memset("pool", W["Sgb"][:, :], 0.0)
        for i in range(nst):
            chunks = self.st_chunks(i)
            nch = len(chunks)
            nt = nch * 128
            t0 = chunks[0] * 128
            last = (i == nst - 1)
            if not last:
                self.stage_A(i + 1, hsrc)
                O.copy("pool", W["hnN"].v3(0, 8, 1, 0, 1), self.hnT[(i + 1) % 2].v3(0, 8, 128, 0, 1))

            def ck(n, key):
                return (key, n)

            if KSTOP == "A0":
                continue
            for ct, nm in ((NAQ, "naq"), (NAK, "nak"), (NAG, "nag")):
                ps = self.inproj_fm(i, ct, nch)
                st = self.stb.next()
                if ct == NAG:
                    O.act(st[:, 0:nt], ps[:, 0:nt], AF.Silu)
                else:
                    O.copy("act", st[:, 0:nt], ps[:, 0:nt])
                if KSTOP != "A1":
                    O.dma(S[nm].v(S[nm].ap[:, t0:t0 + nt], i), st[:, 0:nt])
            if KSTOP in ("A1", "A2"):
                continue
            for ci, n in enumerate(chunks):
                ps = self.HB.next()
                for kc in range(KC):
                    O.mm(ps[:, 0:256], self.hnT[i % 2][:, ci * 1024 + kc * 128:ci * 1024 + (kc + 1) * 128],
                         self.wb[:, kc * NCW_MAX + TMOFF:kc * NCW_MAX + TMOFF + 256], start=(kc == 0), stop=(kc == KC - 1), inc=(kc == KC - 1))
                vt = W["vtm"].next()
                O.copy("act", vt[:, 0:128], ps[:, 0:128])
                O.copy("act", W["gv"][:, ci * 128:(ci + 1) * 128], ps[:, 128:256])
                O.dma(S["nav"].v(S["nav"].ap[n * 128:(n + 1) * 128, :], n), vt[:, 0:128])

            if KSTOP == "A":
                continue
            ps = self.inproj_fm(i, GQ, nch)
            O.ts("dve", W["q2"][:, 0:nt], ps[:, 0:nt], 0.125, mult)
            ps = self.inproj_fm(i, GK, nch)
            O.copy("act", W["k2"][:, 0:nt], ps[:, 0:nt])
            ps = self.inproj_fm(i, GG, nch)
            st = self.stb.next()
            O.act(st[:, 0:nt], ps[:, 0:nt], AF.Silu)
            O.dma(S["gSG"].v(S["gSG"].ap[:, t0:t0 + nt], i), st[:, 0:nt])
            ps = self.inproj_fm(i, GD, nch, M=32)
            O.copy("act", W["gdF"][0:32, 0:nt], ps[0:32, 0:nt])
            ps = self.HB.next()
            O.mm(ps[:, 0:nt], self.gupF[0:32, :], W["gdF"][0:32, 0:nt])
            O.act(W["epg"][:, 0:nt], ps[:, 0:nt], AF.Exp, bias=pvc("negb"), scale=-1.0)
            O.act(W["eng"][:, 0:nt], W["epg"][:, 0:nt], AF.Ln, bias=1.0)
            if i == 0:
                O.stt("dve", W["lwg"][:, 0:nt], W["eng"][:, 0:nt], -1.0 / 16.0, cmv(CM_PAD, 128), mult, mult)
            else:
                O.ts("dve", W["lwg"][:, 0:nt], W["eng"][:, 0:nt], -1.0 / 16.0, mult)
            for ci in range(nch):
                O.scan(W["cumg"][:, ci * 128:(ci + 1) * 128], cmv(CM_ONE, 128), W["lwg"][:, ci * 128:(ci + 1) * 128])
            O.copy("dve", W["totg"].v3(0, nch, 1, 0, 1), W["cumg"].v3(0, nch, 128, 127, 128))
            if i > 0:
                for ci in range(nch):
                    cs = slice(ci * 128, (ci + 1) * 128)
                    O.stt("dve", W["cumg"][64:128, cs], W["lwg"][64:128, cs], W["totg"][64:128, ci:ci + 1], W["cumg"][64:128, cs], add, sub)
            O.act(W["epg"][:, 0:nt], W["cumg"][:, 0:nt], AF.Exp)
            O.act(W["eng"][:, 0:nt], W["cumg"][:, 0:nt], AF.Exp, scale=-1.0)
            for ci in range(nch):
                cs = slice(ci * 128, (ci + 1) * 128)
                O.act(W["erg"][:, cs], W["cumg"][:, cs], AF.Exp, bias=W["totg"][:, ci:ci + 1], scale=-1.0)
            O.act(W["etotg"][:, 0:nch], W["totg"][:, 0:nch], AF.Exp)
            O.tt("dve", W["QhT"][:, 0:nt], W["q2"][:, 0:nt], W["epg"][:, 0:nt], mult)
            O.tt("pool", W["KhTg"][:, 0:nt], W["k2"][:, 0:nt], W["eng"][:, 0:nt], mult)
            O.tt("pool", W["KtTg"][:, 0:nt], W["k2"][:, 0:nt], W["erg"][:, 0:nt], mult)
            for ci, n in enumerate(chunks if KSTOP != "GE" else []):
                cs = slice(ci * 128, (ci + 1) * 128)
                ps = self.HB.next()
                for d in range(2):
                    O.mm(ps[:, d * 128:(d + 1) * 128], W["KhTg"][d * 64:(d + 1) * 64, cs], W["QhT"][d * 64:(d + 1) * 64, cs], inc=(d == 1))
                O.tt("dve", W["AT"][:, :], ps[:, 0:256], cmv(CM_GFF if n == 0 else CM_GMIX, 256), mult)
                if KSTOP == "G1":
                    continue
                pso = self.HB.next()
                O.mm(pso[:, 0:128], W["gv"][:, cs], W["AT"][:, 0:128], start=True, stop=False, inc=False)
                O.mm(pso[:, 0:128], W["gv"][:, cs], W["AT"][:, 128:256], start=False, stop=(n == 0), inc=(n == 0))
                if n > 0:
                    O.mm(pso[:, 0:128], W["Sgb"][0:64, :], W["QhT"][0:64, cs], start=False, stop=True)
                sf = self.stf.next()
                O.copy("act", sf[:, 0:128], pso[:, 0:128])
                O.dma(S["gY1"].v(S["gY1"].ap[:, n * 128:(n + 1) * 128], n), sf[:, 0:128])
                if KSTOP == "G2":
                    continue
                O.tr(self.T1[:, 896:1024], W["KtTg"][:, cs], self.identB[:, :])
                O.copy("act", W["ktm"][:, :], self.T1[:, 896:1024])
                psd = self.HB.next()
                for d in range(2):
                    O.mm(psd[d * 64:(d + 1) * 64, 0:128], W["ktm"][:, d * 64:(d + 1) * 64], W["gv"][:, cs], inc=(d == 1))
                if KSTOP == "G3":
                    continue
                O.stt("dve", W["Sg"][0:64, :], W["Sg"][0:64, :], W["etotg"][0:64, ci:ci + 1], psd[0:64, 0:128], mult, add)
                O.copy("pool", W["Sgb"][0:64, :], W["Sg"][0:64, :])
                if n == 0:
                    O.copy("act", W["Sg"][64:128, :], psd[64:128, 0:128])
                    O.copy("pool", W["Sgb"][64:128, :], W["Sg"][64:128, :])
                else:
                    g = W["gst"].next()
                    O.copy("act", g[64:128, 0:128], psd[64:128, 0:128])
                    O.copy("pool", g[64:128, 128:129], W["etotg"][64:128, ci:ci + 1])
                    O.dma(S["gD"].v(S["gD"].ap[n, 64:128, 0:129], n), g[64:128, 0:129])
                    O.dma(S["gQT"].v(S["gQT"].ap[64:128, n * 128:(n + 1) * 128], n), W["QhT"][64:128, cs])

            if KSTOP in ("G", "GE", "G1", "G2", "G3"):
                continue
            shift = SHIFT_CTS[l]
            dsts = {RR_: W["rT"], RK: W["kT"], RV: W["vT"], RWD: W["tmpa"], RAD: W["adb"], 13: W["vo0"], 14: W["vo1"], 15: W["vo2"]}
            psn = None
            if not last:
                psn = self.HB.next()
                for j, ct in enumerate(shift):
                    for kc in range(KC):
                        O.mm(psn[:, j:j + 1], self.wb[:, kc * NCW_MAX + ct * 128:kc * NCW_MAX + (ct + 1) * 128], W["hnN"][:, kc:kc + 1],
                             start=(kc == 0), stop=(kc == KC - 1), inc=(kc == KC - 1))
                O.copy("act", W["nxt"][:, 0:8], psn[:, 0:8])
            for j, ct in enumerate(shift):
                ps = self.inproj_fm(i, ct, nch)
                z = W["zext"].next()
                O.copy("act", z[:, 1:nt + 1], ps[:, 0:nt])
                O.copy("pool", z[:, 0:1], W["prevcol"][:, j:j + 1])
                if last:
                    O.memset("pool", z[:, nt + 1:nt + 2], 0.0)
                else:
                    O.copy("pool", z[:, nt + 1:nt + 2], W["nxt"][:, j:j + 1])
                O.copy("pool", W["prevcol"][:, j:j + 1], z[:, nt:nt + 1])
                dst = dsts[ct]
                cw = [self.pv[:, PV["cw%d_%d" % (j, t)]:PV["cw%d_%d" % (j, t)] + 1] for t in range(3)]
                O.ts("dve", W["tmpb"][:, 0:nt], z[:, 0:nt], cw[0], mult)
                O.stt("dve", W["tmpb"][:, 0:nt], z[:, 1:nt + 1], cw[1], W["tmpb"][:, 0:nt], mult, add)
                O.stt("dve", dst[:, 0:nt], z[:, 2:nt + 2], cw[2], W["tmpb"][:, 0:nt], mult, add)
            O.act(W["tw"][:, 0:nt], W["tmpa"][:, 0:nt], AF.Tanh)
            ps = self.inproj_fm(i, RG, nch)
            st = self.stb.next()
            O.act(st[:, 0:nt], ps[:, 0:nt], AF.Silu)
            O.dma(S["rSG"].v(S["rSG"].ap[:, t0:t0 + nt], i), st[:, 0:nt])
            if l == 1:
                O.copy("pool", W["vTb"][:, 0:nt], W["vT"][:, 0:nt])
                ps = self.HB.next()
                srcs = [W["vTb"], W["vo0"], W["vo1"], W["vo2"]]
                for j in range(4):
                    O.mm(ps[0:32, 0:nt], self.vdnB[:, j * 32:(j + 1) * 32], srcs[j][:, 0:nt], start=(j == 0), stop=(j == 3), inc=(j == 3))
                O.copy("act", W["vdb"][0:32, 0:nt], ps[0:32, 0:nt])
                ps = self.HB.next()
                O.mm(ps[:, 0:nt], self.vupB[0:32, :], W["vdb"][0:32, 0:nt])
                O.act(W["tmpa"][:, 0:nt], ps[:, 0:nt], AF.Sigmoid, bias=pvc("v0"))
                O.dma(W["vf"][:, 0:nt], S["vfirst"].v(S["vfirst"].ap[:, t0:t0 + nt], i))
                O.tt("pool", W["vf"][:, 0:nt], W["vf"][:, 0:nt], W["vT"][:, 0:nt], sub)
                O.tt("dve", W["vf"][:, 0:nt], W["vf"][:, 0:nt], W["tmpa"][:, 0:nt], mult)
                O.tt("dve", W["vT"][:, 0:nt], W["vT"][:, 0:nt], W["vf"][:, 0:nt], add)
            else:
                O.dma(S["vfirst"].v(S["vfirst"].ap[:, t0:t0 + nt], i), W["vT"][:, 0:nt])
            O.copy("pool", W["vTb"][:, 0:nt], W["vT"][:, 0:nt])
            O.ts("dve", W["kr"][:, 0:nt], W["kT"][:, 0:nt], pvc("k_k"), mult)
            O.tt("pool", W["tmpa"][:, 0:nt], W["kr"][:, 0:nt], W["kr"][:, 0:nt], mult)
            ps = self.HB.next()
            O.mm(ps[:, 0:nt], cmv(CM_BO, 128), W["tmpa"][:, 0:nt])
            O.act(W["inv"][:, 0:nt], ps[:, 0:nt], AF.Sqrt)
            O.ts("dve", W["inv"][:, 0:nt], W["inv"][:, 0:nt], 1e-12, ALU.max)
            O.recip(W["inv"][:, 0:nt], W["inv"][:, 0:nt])
            O.tt("dve", W["kkn"][:, 0:nt], W["kr"][:, 0:nt], W["inv"][:, 0:nt], mult)
            for d in range(2):
                lw, al, km, bb = W["lw%d" % d], W["al%d" % d], W["km%d" % d], W["b%d" % d]
                hs = slice(d * 64, (d + 1) * 64)
                ps = self.HB.next()
                O.mm(ps[:, 0:nt], self.wupB[hs, :], W["tw"][hs, 0:nt])
                O.act(W["tmpa"][:, 0:nt], ps[:, 0:nt], AF.Sigmoid, bias=pvc("w0_%d" % d))
                if i == 0:
                    O.stt("dve", lw[:, 0:nt], W["tmpa"][:, 0:nt], -EM05, cmv(CM_PAD, 128), mult, mult)
                else:
                    O.ts("pool", lw[:, 0:nt], W["tmpa"][:, 0:nt], -EM05, mult)
                ps = self.HB.next()
                O.mm(ps[:, 0:nt], self.aupB[hs, :], W["adb"][hs, 0:nt])
                O.act(al[:, 0:nt], ps[:, 0:nt], AF.Sigmoid, bias=pvc("a0_%d" % d))
                O.ts("dve", W["tmpb"][:, 0:nt], al[:, 0:nt], pvc("k_a"), mult, pvc("omka"), add)
                O.tt("dve", km[:, 0:nt], W["tmpb"][:, 0:nt], W["kT"][:, 0:nt], mult)
                O.tt("pool", bb[:, 0:nt], W["kkn"][:, 0:nt], al[:, 0:nt], mult)
            O.tt("pool", W["tmpa"][:, 0:nt], W["km0"][:, 0:nt], W["km1"][:, 0:nt], add)
            O.stt("dve", W["tmpa"][:, 0:nt], W["tmpa"][:, 0:nt], pvc("r_k"), W["rT"][:, 0:nt], mult, mult)
            ps = self.HB.next()
            O.mm(ps[:, 0:nt], cmv(CM_BO, 128), W["tmpa"][:, 0:nt])
            sf = self.stf.next()
            O.stt("dve", sf[:, 0:nt], ps[:, 0:nt], 0.5, W["vT"][:, 0:nt], mult, mult)
            O.dma(S["rBV"].v(S["rBV"].ap[:, t0:t0 + nt], i), sf[:, 0:nt])
            for d in range(2):
                lw, km, bb, cum, tot, etot = W["lw%d" % d], W["km%d" % d], W["b%d" % d], W["cum%d" % d], W["tot%d" % d], W["etot%d" % d]
                AR = W["AR%d" % d]
                for ci in range(nch):
                    cs = slice(ci * 128, (ci + 1) * 128)
                    O.scan(cum[:, cs], cmv(CM_ONE, 128), lw[:, cs])
                O.copy("dve", tot.v3(0, nch, 1, 0, 1), cum.v3(0, nch, 128, 127, 128))
                if d == 1 and i > 0:
                    for ci in range(nch):
                        cs = slice(ci * 128, (ci + 1) * 128)
                        O.stt("dve", cum[:, cs], lw[:, cs], tot[:, ci:ci + 1], cum[:, cs], add, sub)
                O.tt("pool", W["cx"][:, 0:nt], cum[:, 0:nt], lw[:, 0:nt], sub)
                O.act(W["eex"][:, 0:nt], W["cx"][:, 0:nt], AF.Exp)
                O.act(W["epos"][:, 0:nt], cum[:, 0:nt], AF.Exp)
                O.act(W["eneg"][:, 0:nt], cum[:, 0:nt], AF.Exp, scale=-1.0)
                for ci in range(nch):
                    cs = slice(ci * 128, (ci + 1) * 128)
                    O.act(W["erem"][:, cs], cum[:, cs], AF.Exp, bias=tot[:, ci:ci + 1], scale=-1.0)
                O.act(etot[:, 0:nch], tot[:, 0:nch], AF.Exp)
                O.stt("dve", AR.v3(0, nch, 256, 0, 128), W["kkn"].v3(0, nch, 128, 0, 128), -1.0, W["eex"].v3(0, nch, 128, 0, 128), mult, mult)
                O.tt("dve", AR.v3(0, nch, 256, 128, 256), W["rT"].v3(0, nch, 128, 0, 128), W["epos"].v3(0, nch, 128, 0, 128), mult)
                O.tt("pool", W["BhT%d" % d][:, 0:nt], bb[:, 0:nt], W["eneg"][:, 0:nt], mult)
                O.tt("dve", W["KhT%d" % d][:, 0:nt], km[:, 0:nt], W["eneg"][:, 0:nt], mult)
                O.tt("pool", W["BtT%d" % d][:, 0:nt], bb[:, 0:nt], W["erem"][:, 0:nt], mult)
                O.tt("dve", W["KtT%d" % d][:, 0:nt], km[:, 0:nt], W["erem"][:, 0:nt], mult)
            for ci, n in enumerate(chunks if KSTOP != "R" else []):
                self.rwkv_chunk(ci, n)

    def rwkv_chunk(self, ci, n):
        O, W, S = self.O, self.W, self.S
        cmv = self.cmv
        mult, add = ALU.mult, ALU.add
        cs = slice(ci * 128, (ci + 1) * 128)
        tk = W["tk"].next()
        for d in range(2):
            O.tr(self.T1[:, (3 * d) * 128:(3 * d + 1) * 128], W["AR%d" % d][:, ci * 256:ci * 256 + 128], self.identB[:, :], inc=False)
            O.tr(self.T1[:, (3 * d + 1) * 128:(3 * d + 2) * 128], W["BtT%d" % d][:, cs], self.identB[:, :], inc=False)
            O.tr(self.T1[:, (3 * d + 2) * 128:(3 * d + 3) * 128], W["KtT%d" % d][:, cs], self.identB[:, :], inc=False)
        O.tr(self.T1[:, 768:896], W["vTb"][:, cs], self.identB[:, :])
        O.copy("act", tk[:, 0:896], self.T1[:, 0:896])
        st = [None, None]
        for d in range(2):
            fwdtype = (d == 0 or n == 0)
            AR, BhT, KhT = W["AR%d" % d], W["BhT%d" % d], W["KhT%d" % d]
            SC1, SC2 = W["SC1"][d], W["SC2"][d]
            RX = W["RX"][d].next()
            ps1, ps2, psl = self.FB.next(), self.FB.next(), self.HB.next()
            for h in range(2):
                hs = slice(h * 64, (h + 1) * 64)
                O.mm(ps1[:, h * 256:(h + 1) * 256], BhT[hs, cs], AR[hs, ci * 256:(ci + 1) * 256], inc=(h == 1))
            for h in range(2):
                hs = slice(h * 64, (h + 1) * 64)
                O.mm(ps2[:, h * 256:(h + 1) * 256], KhT[hs, cs], AR[hs, ci * 256:(ci + 1) * 256], inc=(h == 1))
            for h in range(2):
                hs = slice(h * 64, (h + 1) * 64)
                O.mm(psl[:, h * 128:(h + 1) * 128], AR[hs, ci * 256:ci * 256 + 128], BhT[hs, cs], inc=(h == 1))
            M2 = cmv(CM_MF2 if fwdtype else CM_MB2, 512)
            O.tt("dve", SC1[:, :], ps1[:, :], M2, mult)
            O.tt("dve", SC2[:, :], ps2[:, :], M2, mult)
            LMoff = CM_SL2 if fwdtype else CM_SU2
            O.tt("dve", RX.v3(0, 2, 256, 0, 128), psl.v3(0, 2, 128, 0, 128), self.cm.v3(LMoff, 2, 128, 0, 128), mult)
            O.copy("pool", RX.v3(0, 2, 256, 128, 192), tk.v3(3 * d * 128, 2, 64, 0, 64))
            psw = self.HB.next()
            for h in range(2):
                O.mm(psw[:, h * 64:(h + 1) * 64], SC2[:, h * 256:h * 256 + 128], tk[:, 768 + h * 64:768 + (h + 1) * 64], inc=(h == 1))
            O.copy("act", RX.v3(0, 2, 256, 192, 256), psw.v3(0, 2, 64, 0, 64))
            st[d] = dict(RX=RX, LT=SC1, lt_off=0, lt_stride=256)
        for j in range(7):
            lastl = (j == 6)
            for d in range(2):
                X = st[d]
                RX, LT, lt_off, lt_stride = X["RX"], X["LT"], X["lt_off"], X["lt_stride"]
                PS = self.FB.next()
                for h in range(2):
                    lt = LT[:, lt_off + h * lt_stride:lt_off + h * lt_stride + 128]
                    if lastl:
                        O.mm(PS[:, h * 256 + 128:(h + 1) * 256], lt, RX[:, h * 256 + 128:(h + 1) * 256], inc=(h == 1))
                    else:
                        O.mm(PS[:, h * 256:(h + 1) * 256], lt, RX[:, h * 256:(h + 1) * 256], inc=(h == 1))
                RXn = W["RX"][d].next()
                if not lastl:
                    PST = self.HB.next()
                    for h in range(2):
                        lt = LT[:, lt_off + h * lt_stride:lt_off + h * lt_stride + 128]
                        O.mm(PST[:, h * 128:(h + 1) * 128], RX[:, h * 256:h * 256 + 128], lt, inc=(h == 1))
                    LTn = W["LTn"][d].next()
                    O.copy("act", LTn[:, :], PST[:, 0:256])
                if j < 5:
                    O.copy("act", RXn.v3(0, 2, 256, 0, 128), PS.v3(0, 2, 256, 0, 128))
                O.tt("dve", RXn.v3(0, 2, 256, 128, 256), PS.v3(0, 2, 256, 128, 256), RX.v3(0, 2, 256, 128, 256), add)
                X["RX"] = RXn
                if not lastl:
                    X["LT"], X["lt_off"], X["lt_stride"] = LTn, 0, 128
        fin = []
        for d in range(2):
            AR = W["AR%d" % d]
            SC1 = W["SC1"][d]
            RX = st[d]["RX"]
            psr = self.HB.next()
            for h in range(2):
                O.mm(psr[h * 64:(h + 1) * 64, 0:128], RX[:, h * 256 + 128:h * 256 + 192], SC1[:, h * 256 + 128:(h + 1) * 256], inc=(h == 1))
            RpT = W["RpT"][d]
            O.tt("dve", RpT[:, :], psr[:, 0:128], AR[:, ci * 256 + 128:(ci + 1) * 256], add)
            psp = self.HB.next()
            etot = W["etot%d" % d]
            for h in range(2):
                hp = slice(h * 64, (h + 1) * 64)
                Ap = RX[:, h * 256 + 128:h * 256 + 192]
                U0 = RX[:, h * 256 + 192:h * 256 + 256]
                Bt = tk[:, (3 * d + 1) * 128 + h * 64:(3 * d + 1) * 128 + (h + 1) * 64]
                Kt = tk[:, (3 * d + 2) * 128 + h * 64:(3 * d + 2) * 128 + (h + 1) * 64]
                Vh = tk[:, 768 + h * 64:768 + (h + 1) * 64]
                O.mm(psp[hp, 0:64], Ap, Bt, inc=False)
                O.mm(psp[hp, 64:128], Bt, U0, start=True, stop=False, inc=False)
                O.mm(psp[hp, 64:128], Kt, Vh, start=False, stop=True, inc=(h == 1))
            PD = W["PD"][d].next()
            for h in range(2):
                hp = slice(h * 64, (h + 1) * 64)
                O.stt("dve", PD[hp, 0:64], self.cm[hp, CM_ID + h * 64:CM_ID + (h + 1) * 64], etot[hp, ci:ci + 1], psp[hp, 0:64], mult, add)
            O.copy("act", PD[:, 64:128], psp[:, 64:128])
            fin.append((RX, PD, RpT))
        psy = self.HB.next()
        for h in range(2):
            hp = slice(h * 64, (h + 1) * 64)
            Vh = tk[:, 768 + h * 64:768 + (h + 1) * 64]
            for d in range(2):
                RX = fin[d][0]
                U0 = RX[:, h * 256 + 192:h * 256 + 256]
                O.mm(psy[hp, 0:128], U0, W["SC1"][d][:, h * 256 + 128:(h + 1) * 256], start=(d == 0), stop=False, inc=False)
                O.mm(psy[hp, 0:128], Vh, W["SC2"][d][:, h * 256 + 128:(h + 1) * 256], start=False, stop=(d == 1 and n == 0),
                     inc=(d == 1 and n == 0 and h == 1))
            if n > 0:
                O.mm(psy[hp, 0:128], W["Sfb"][hp, 0:64], fin[0][2][hp, :], start=False, stop=True, inc=(h == 1))
        sf = self.stf.next()
        O.copy("act", sf[:, 0:128], psy[:, 0:128])
        O.dma(S["rY1"].v(S["rY1"].ap[:, n * 128:(n + 1) * 128], n), sf[:, 0:128])
        PD0, PD1 = fin[0][1], fin[1][1]
        if n == 0:
            O.copy("dve", W["Sf"][:, :], PD0[:, 64:128])
            O.copy("dve", W["Sb"][:, :], PD1[:, 64:128])
            O.copy("pool", W["Sbb"][:, :], W["Sb"][:, :])
        else:
            pss = self.HB.next()
            for h in range(2):
                hp = slice(h * 64, (h + 1) * 64)
                O.mm(pss[hp, 0:64], PD0[hp, 0:64], W["Sf"][hp, 0:64], inc=(h == 1))
            O.tt("dve", W["Sf"][:, :], pss[:, 0:64], PD0[:, 64:128], add)
            O.dma(S["rPD"].v(S["rPD"].ap[n, :, :], n), PD1[:, :])
            O.dma(S["rRT"].v(S["rRT"].ap[:, n * 128:(n + 1) * 128], n), fin[1][2][:, :])
        O.copy("pool", W["Sfb"][:, :], W["Sf"][:, :])

    def setup_sweep2(self):
        sb = self.sb
        X = {}
        X["y1"] = RR([sb("y1_%d" % i, [128, 128]) for i in range(2)])
        X["g1"] = RR([sb("g1_%d" % i, [128, 128]) for i in range(2)])
        X["rt"] = RR([sb("rt_%d" % i, [128, 128], BF16) for i in range(2)])
        X["pd"] = RR([sb("pd_%d" % i, [128, 128]) for i in range(2)])
        X["gq"] = RR([sb("gq_%d" % i, [128, 128], BF16) for i in range(2)])
        X["gd"] = RR([sb("gd_%d" % i, [128, 132]) for i in range(2)])
        X["yr"] = sb("yr", [128, NT])
        X["yg"] = sb("yg", [128, NT])
        X["bv"] = sb("bv", [128, NT])
        X["sgr"] = sb("sgr", [128, NT], BF16)
        X["sgg"] = sb("sgg", [128, NT], BF16)
        for n in ("f1", "f2", "f3", "f4"):
            X[n] = sb(n, [128, NT])
        X["ob"] = RR([sb("ob%d" % i, [128, NT], BF16) for i in range(2)])
        self.X = X
        self.W = dict(self.Wp)

    def sweep2(self, l, odst):
        O, W, S, X = self.O, self.W, self.S, self.X
        cmv, pvc = self.cmv, self.pvc
        mult, add, sub = ALU.mult, ALU.add, ALU.subtract
        nst = self.nreal // 2 + 1
        for i in list(range(nst - 1, 0, -1)) + [0]:
            chunks = self.st_chunks(i)
            nch = len(chunks)
            nt = nch * 128
            t0 = chunks[0] * 128
            O.dma(X["bv"][:, 0:nt], S["rBV"].v(S["rBV"].ap[:, t0:t0 + nt], i))
            O.dma(X["sgr"][:, 0:nt], S["rSG"].v(S["rSG"].ap[:, t0:t0 + nt], i))
            O.dma(X["sgg"][:, 0:nt], S["gSG"].v(S["gSG"].ap[:, t0:t0 + nt], i))
            for ci in reversed(range(nch)):
                n = chunks[ci]
                cs = slice(ci * 128, (ci + 1) * 128)
                y1, g1 = X["y1"].next(), X["g1"].next()
                O.dma(y1[:, :], S["rY1"].v(S["rY1"].ap[:, n * 128:(n + 1) * 128], n))
                O.dma(g1[:, :], S["gY1"].v(S["gY1"].ap[:, n * 128:(n + 1) * 128], n))
                if n == 0:
                    O.copy("pool", X["yr"][:, cs], y1[:, :])
                    O.copy("pool", X["yg"][:, cs], g1[:, :])
                    continue
                rt, pd, gq, gd = X["rt"].next(), X["pd"].next(), X["gq"].next(), X["gd"].next()
                O.dma(rt[:, :], S["rRT"].v(S["rRT"].ap[:, n * 128:(n + 1) * 128], n))
                O.dma(pd[:, :], S["rPD"].v(S["rPD"].ap[n, :, :], n))
                O.dma(gq[64:128, :], S["gQT"].v(S["gQT"].ap[64:128, n * 128:(n + 1) * 128], n))
                O.dma(gd[64:128, 0:129], S["gD"].v(S["gD"].ap[n, 64:128, 0:129], n))
                psy = self.HB.next()
                for h in range(2):
                    hp = slice(h * 64, (h + 1) * 64)
                    O.mm(psy[hp, 0:128], W["Sbb"][hp, 0:64], rt[hp, :], inc=(h == 1))
                O.tt("dve", X["yr"][:, cs], psy[:, 0:128], y1[:, :], add)
                pss = self.HB.next()
                for h in range(2):
                    hp = slice(h * 64, (h + 1) * 64)
                    O.mm(pss[hp, 0:64], pd[hp, 0:64], W["Sb"][hp, 0:64], inc=(h == 1))
                O.tt("dve", W["Sb"][:, :], pss[:, 0:64], pd[:, 64:128], add)
                O.copy("pool", W["Sbb"][:, :], W["Sb"][:, :])
                pso = self.HB.next()
                O.mm(pso[:, 0:128], W["Sgb"][64:128, :], gq[64:128, :])
                O.tt("dve", X["yg"][:, cs], pso[:, 0:128], g1[:, :], add)
                O.stt("dve", W["Sg"][64:128, :], W["Sg"][64:128, :], gd[64:128, 128:129], gd[64:128, 0:128], mult, add)
                O.copy("pool", W["Sgb"][64:128, :], W["Sg"][64:128, :])
            yr, f1, f2, f3 = X["yr"], X["f1"], X["f2"], X["f3"]
            ps = self.HB.next()
            O.mm(ps[:, 0:nt], cmv(CM_BO, 128), yr[:, 0:nt])
            O.stt("dve", f1[:, 0:nt], ps[:, 0:nt], -1.0 / 64.0, yr[:, 0:nt], mult, add)
            O.tt("pool", f2[:, 0:nt], f1[:, 0:nt], f1[:, 0:nt], mult)
            ps = self.HB.next()
            O.mm(ps[:, 0:nt], cmv(CM_BO, 128), f2[:, 0:nt])
            O.act(f3[:, 0:nt], ps[:, 0:nt], AF.Sqrt, bias=64e-5, scale=1.0 / 64.0)
            O.recip(f3[:, 0:nt], f3[:, 0:nt])
            O.tt("dve", f1[:, 0:nt], f1[:, 0:nt], f3[:, 0:nt], mult)
            O.ts("dve", f1[:, 0:nt], f1[:, 0:nt], pvc("ln_g"), mult, pvc("ln_b"), add)
            O.tt("pool", f1[:, 0:nt], f1[:, 0:nt], X["bv"][:, 0:nt], add)
            ob = X["ob"].next()
            O.tt("dve", ob[:, 0:nt], f1[:, 0:nt], X["sgr"][:, 0:nt], mult)
            for ci, n in enumerate(chunks):
                for dv in odst(2, n):
                    O.dma(dv, ob[:, ci * 128:(ci + 1) * 128])
            yg, f4 = X["yg"], X["f4"]
            O.tt("pool", f2[:, 0:nt], yg[:, 0:nt], yg[:, 0:nt], mult)
            ps = self.HB.next()
            O.mm(ps[:, 0:nt], cmv(CM_ONE, 128), f2[:, 0:nt])
            O.act(f4[:, 0:nt], ps[:, 0:nt], AF.Sqrt, bias=1e-5, scale=1.0 / 128.0)
            O.recip(f4[:, 0:nt], f4[:, 0:nt])
            O.tt("dve", f4[:, 0:nt], f4[:, 0:nt], yg[:, 0:nt], mult)
            ob = X["ob"].next()
            O.stt("dve", ob[:, 0:nt], f4[:, 0:nt], pvc("gng"), X["sgg"][:, 0:nt], mult, mult)
            for ci, n in enumerate(chunks):
                for dv in odst(1, n):
                    O.dma(dv, ob[:, ci * 128:(ci + 1) * 128])

    def alloc_NA(self):
        sb = self.sb
        N = {}
        N["QT"] = RR([sb("naQT%d" % i, [128, 256], BF16) for i in range(2)])
        N["sg"] = RR([sb("nasg%d" % i, [128, 256], BF16) for i in range(2)])
        N["KT"] = RR([sb("naKT%d" % i, [128, 768], BF16) for i in range(2)])
        N["Vt"] = RR([sb("naVt%d" % i, [128, 768], BF16) for i in range(2)])
        N["KmT"] = sb("naKmT", [128, 16], BF16)
        N["Vm"] = sb("naVm", [16, 128], BF16)
        N["tmp"] = RR([sb("natmp%d" % i, [128, 512]) for i in range(2)])
        N["PT"] = RR([sb("naPT%d" % i, [128, 512], BF16) for i in range(3)])
        N["PmT"] = sb("naPmT", [16, 512], BF16)
        N["rec"] = sb("narec", [128, 256])
        N["of"] = sb("naof", [128, 256])
        N["ob"] = RR([sb("naob%d" % i, [128, 256], BF16) for i in range(2)])
        self.N = N

    def na_phase(self, odst):
        O, S, N = self.O, self.S, self.N
        mult, add = ALU.mult, ALU.add
        O.dma(N["KmT"][:, :], S["nak"].v(S["nak"].ap[:, 112:128], "na"))
        O.dma(N["Vm"][:, :], S["nav"].v(S["nav"].ap[112:128, :], "na"))
        QT, sg = N["QT"].next(), N["sg"].next()
        O.dma(QT[:, 0:16], S["naq"].v(S["naq"].ap[:, 112:128], "na"))
        O.dma(sg[:, 0:16], S["nag"].v(S["nag"].ap[:, 112:128], "na"))
        psm = self.FB.next()
        for h in range(2):
            hs = slice(h * 64, (h + 1) * 64)
            O.mm(psm[0:16, h * 16:(h + 1) * 16], N["KmT"][hs, 0:16], QT[hs, 0:16], inc=(h == 1))
        O.act(N["PmT"][0:16, 0:32], psm[0:16, 0:32], AF.Exp, scale=0.125)
        psO, psD = self.HB.next(), self.HB.next()
        for h in range(2):
            hs = slice(h * 64, (h + 1) * 64)
            O.mm(psO[hs, 0:16], N["Vm"][0:16, hs], N["PmT"][0:16, h * 16:(h + 1) * 16], inc=False)
            O.mm(psD[hs, 0:16], self.onesB[0:16, 0:64], N["PmT"][0:16, h * 16:(h + 1) * 16], inc=(h == 1))
        O.recip(N["rec"][:, 0:16], psD[:, 0:16])
        O.tt("dve", N["of"][:, 0:16], psO[:, 0:16], N["rec"][:, 0:16], mult)
        ob = N["ob"].next()
        O.memset("pool", ob[:, 0:128], 0.0)
        O.tt("pool", ob[:, 112:128], N["of"][:, 0:16], sg[:, 0:16], mult)
        for dv in odst(0, 0):
            O.dma(dv, ob[:, 0:128])
        for B, lst in enumerate(self.blocks):
            c0 = 1 + 2 * B
            QT, sg, KT, Vt = N["QT"].next(), N["sg"].next(), N["KT"].next(), N["Vt"].next()
            O.dma(QT[:, :], S["naq"].v(S["naq"].ap[:, c0 * 128:(c0 + 2) * 128], "na"))
            O.dma(sg[:, :], S["nag"].v(S["nag"].ap[:, c0 * 128:(c0 + 2) * 128], "na"))
            plo, phi = lst[0][0], lst[-1][0]
            nk = phi - plo + 1
            assert nk == len(lst) and nk <= 6
            O.dma(KT[:, 0:nk * 128], S["nak"].v(S["nak"].ap[:, (plo + 1) * 128:(phi + 2) * 128], "na"))
            O.dma(Vt.v3(0, nk, 128, 0, 128),
                  S["nav"].v(S["nav"].ap[(plo + 1) * 128:(phi + 2) * 128, :].rearrange("(n p) d -> p n d", p=128), "na"))
            psO, psD = self.HB.next(), self.HB.next()
            for ti, (p, cfg) in enumerate(lst):
                ps = self.FB.next()
                for h in range(2):
                    hs = slice(h * 64, (h + 1) * 64)
                    O.mm(ps[:, h * 256:(h + 1) * 256], KT[hs, ti * 128:(ti + 1) * 128], QT[hs, 0:256], inc=(h == 1))
                tmp, PT = N["tmp"].next(), N["PT"].next()
                O.stt("dve", tmp[:, :], ps[:, :], 0.125, self.nab[:, cfg * 512:(cfg + 1) * 512], mult, add)
                O.act(PT[:, :], tmp[:, :], AF.Exp)
                for h in range(2):
                    hs = slice(h * 64, (h + 1) * 64)
                    O.mm(psO[hs, 0:256], Vt[:, ti * 128 + h * 64:ti * 128 + (h + 1) * 64], PT[:, h * 256:(h + 1) * 256],
                         start=(ti == 0), stop=False, inc=False)
                    O.mm(psD[hs, 0:256], self.onesB[:, 0:64], PT[:, h * 256:(h + 1) * 256], start=(ti == 0), stop=False, inc=False)
            psm = self.FB.next()
            for h in range(2):
                hs = slice(h * 64, (h + 1) * 64)
                O.mm(psm[0:16, h * 256:(h + 1) * 256], N["KmT"][hs, 0:16], QT[hs, 0:256], inc=(h == 1))
            O.act(N["PmT"][0:16, :], psm[0:16, :], AF.Exp, scale=0.125)
            for h in range(2):
                hs = slice(h * 64, (h + 1) * 64)
                O.mm(psO[hs, 0:256], N["Vm"][0:16, hs], N["PmT"][0:16, h * 256:(h + 1) * 256], start=False, stop=True, inc=False)
                O.mm(psD[hs, 0:256], self.onesB[0:16, 0:64], N["PmT"][0:16, h * 256:(h + 1) * 256], start=False, stop=True, inc=(h == 1))
            O.recip(N["rec"][:, :], psD[:, 0:256])
            O.tt("dve", N["of"][:, :], psO[:, 0:256], N["rec"][:, :], mult)
            ob = N["ob"].next()
            O.tt("pool", ob[:, :], N["of"][:, :], sg[:, :], mult)
            for ci in range(2):
                for dv in odst(0, c0 + ci):
                    O.dma(dv, ob[:, ci * 128:(ci + 1) * 128])

    def alloc_O(self):
        sb = self.sb
        Q = {}
        Q["wob"] = sb("wob", [128, 12 * 1024], BF16)
        Q["fg"] = sb("fgbc", [128, 1024])
        Q["oT"] = RR([sb("oT%d" % i, [128, 12 * 128], BF16) for i in range(2)])
        Q["hx"] = RR([sb("ohx%d" % i, [128, 1024]) for i in range(2)])
        Q["hn"] = RR([sb("ohn%d" % i, [128, 1024]) for i in range(2)])
        Q["ot"] = RR([sb("oot%d" % i, [128, 1024]) for i in range(2)])
        Q["junk"] = sb("ojunk", [128, 1024], BF16)
        Q["ss"] = RR([sb("oss%d" % i, [128, 4]) for i in range(2)])
        self.Q = Q

    def o_phase(self, l, wo, fg, osrc, hres, hdst, outdst):
        O, Q = self.O, self.Q
        mult, add = ALU.mult, ALU.add
        final = outdst is not None
        for kc in range(12):
            self.load_cast(Q["wob"], kc * 1024, wo, lambda c, w, kc=kc: wo.ap[kc * 128:(kc + 1) * 128, c:c + w], 1024, 0)
        if final:
            O.dma(Q["fg"][:, :], fg.v(fg.ap[:, :], 0))
        for m in range(self.ntl):
            oT = Q["oT"].next()
            for i in range(4):
                O.dma(oT.v3(i * 384, 3, 128, 0, 128), osrc(m, i))
            hx = Q["hx"].next()
            O.dma(hx[:, :], hres(m))
            hn = Q["hn"].next()
            for nn in range(2):
                ps = self.FB.next()
                cs = slice(nn * 512, (nn + 1) * 512)
                for kc in range(12):
                    O.mm(ps[:, :], oT[:, kc * 128:(kc + 1) * 128], Q["wob"][:, kc * 1024 + nn * 512:kc * 1024 + (nn + 1) * 512],
                         start=(kc == 0), stop=(kc == 11), inc=(kc == 11))
                if m == 0:
                    O.stt("dve", hn[:, cs], ps[:, :], self.pvc("padrow"), hx[:, cs], mult, add)
                else:
                    O.tt("dve", hn[:, cs], ps[:, :], hx[:, cs], add)
            if not final:
                O.dma(hdst(m), hn[:, :])
            elif m >= 1:
                ss = Q["ss"].next()
                O.act(Q["junk"][:, :], hn[:, :], AF.Square, accum=ss[:, 0:1])
                O.act(ss[:, 1:2], ss[:, 0:1], AF.Sqrt, bias=1e-6, scale=1.0 / DM)
                O.recip(ss[:, 2:3], ss[:, 1:2])
                ot = Q["ot"].next()
                O.stt("dve", ot[:, :], hn[:, :], ss[:, 2:3], Q["fg"][:, :], mult, mult)
                O.dma(outdst(m), ot[:, :])


def build(nreal, phases):
    C = Ctx(nreal)
    has = set(phases)
    ntl, npq = C.ntl, C.npq
    C.setup()
    C.setup_M()
    EI, EO = "ExternalInput", "ExternalOutput"
    D = {}
    if "M0" in has:
        D["hg0"] = C.dram("hg0", [4 * ntl * 128, 1024], F32, EI)
    if "M1" in has:
        D["hg1"] = C.dram("hg1", [4 * ntl * 128, 1024], F32, "Internal" if "O0" in has else EI)
    for l in range(2):
        m, o = "M%d" % l in has, "O%d" % l in has
        if m:
            D["ogi%d" % l] = C.dram("ogi%d" % l, [4 * 384, ntl * 128], BF16, "Internal" if o else EO)
        if o:
            D["ogo%d" % l] = C.dram("ogo%d" % l, [4 * 384, ntl * 128], BF16, "Internal" if m else EI)
            D["wo%d" % l] = C.dram("wo%d" % l, [1536, 1024], F32, EI)
    if "O0" in has:
        D["hmine"] = C.dram("hmine", [ntl * 128, 1024], F32, EI)
        D["hgi"] = C.dram("hgi", [ntl * 128, 1024], F32, "Internal" if "M1" in has else EO)
    elif "O1" in has:
        D["hgi"] = C.dram("hgi", [ntl * 128, 1024], F32, EI)
    if "O1" in has:
        D["fg"] = C.dram("fg", [128, 1024], F32, EI)
        D["out"] = C.dram("out", [npq * 128, 1024], F32, EO)
    if "M0" in has and "M1" not in has:
        C.S["vfirst"] = C.dram("vfirst_o", [128, C.TP], F32, EO)
    if "M1" in has and "M0" not in has:
        C.S["vfirst"] = C.dram("vfirst_i", [128, C.TP], F32, EI)
    C.arena_start()
    groups = [[0, 1, 2, 3], [4, 5, 6, 7]]

    def hsrc(l):
        t = D["hg%d" % l]

        def f(n):
            if n == 0:
                r0 = 0
            else:
                r0 = ((n - 1) // npq) * ntl * 128 + (1 + (n - 1) % npq) * 128
            return t.v(t.ap[r0:r0 + 128, :], ("h", n))
        return f

    def odst(l):
        t = D["ogi%d" % l]

        def f(blk, n):
            if n == 0:
                return [t.v(t.ap[dq * 384 + blk * 128:dq * 384 + (blk + 1) * 128, 0:128], (blk, n, dq)) for dq in range(4)]
            dq, loc = (n - 1) // npq, 1 + (n - 1) % npq
            return [t.v(t.ap[dq * 384 + blk * 128:dq * 384 + (blk + 1) * 128, loc * 128:(loc + 1) * 128], (blk, n, dq))]
        return f

    def osrc(l):
        t = D["ogo%d" % l]
        return lambda m, i: t.v(t.ap[i * 384:(i + 1) * 384, m * 128:(m + 1) * 128].rearrange("(b p) t -> p b t", p=128), ("o", m, i))

    for ph in phases:
        l = int(ph[1])
        if ph[0] == "M":
            C.arena_reset()
            C.load_M_params(l)
            C.alloc_sweep1()
            C.sweep1(l, hsrc(l))
            C.arena_reset()
            if KSTOP in ("", "N", "S"):
                if KSTOP != "S":
                    C.alloc_NA()
                    C.na_phase(odst(l))
                    C.arena_reset()
                if KSTOP != "N":
                    C.setup_sweep2()
                    C.sweep2(l, odst(l))
                    C.arena_reset()
            if "O%d" % l in has:
                gi, go = D["ogi%d" % l], D["ogo%d" % l]
                C.P.collective("AllToAll", groups, gi.v(gi.ap[:, :], "cc"), go.v(go.ap[:, :], "cc"))
                C.P.barrier()
        else:
            C.arena_reset()
            C.alloc_O()
            if l == 0:
                C.O.dma(C.pv[:, :], C.Min[0]["pv"].v(C.Min[0]["pv"].ap[:, :], 1))
            hg, hm = D["hgi"], D.get("hmine")
            hres = (lambda m: hm.v(hm.ap[m * 128:(m + 1) * 128, :], m)) if l == 0 else (lambda m: hg.v(hg.ap[m * 128:(m + 1) * 128, :], ("r", m)))
            hdst = (lambda m: hg.v(hg.ap[m * 128:(m + 1) * 128, :], ("w", m))) if l == 0 else None
            outdst = None
            if l == 1:
                ot = D["out"]
                outdst = lambda m: ot.v(ot.ap[(m - 1) * 128:m * 128, :], m)
            C.o_phase(l, D["wo%d" % l], D.get("fg"), osrc(l), hres, hdst, outdst)
            C.arena_reset()
            if l == 0 and "M1" in has:
                h1 = D["hg1"]
                C.P.collective("AllGather", groups, hg.v(hg.ap[:, :], "cc"), h1.v(h1.ap[:, :], "cc"))
                C.P.barrier()
    C.P.barrier()
    with C.nc.Block() as block:
        C.P.emit(block)
    C.es.close()
    return C


def host_layouts(inp, nreal):
    f = np.float32
    npq, ntl = nreal // 4, nreal // 4 + 1
    T = nreal * 128
    x, meta = inp["x"], inp["meta"]
    cm = const_masks()
    blocks, tiles = na_plan(nreal)
    per = []
    for c in range(8):
        b, q = c // 4, c % 4
        hg = np.zeros((4, ntl * 128, 1024), f)
        for i in range(4):
            hg[i, 112:128] = meta
            hg[i, 128:] = x[b, i * npq * 128:(i + 1) * npq * 128]
        d = {"cmask": cm, "hg0": hg.reshape(4 * ntl * 128, 1024), "hmine": np.ascontiguousarray(hg[q])}
        for l in range(2):
            cp = core_params(inp, l, q)
            for k, v in cp.items():
                d["%s%d" % (k, l)] = np.ascontiguousarray(v, dtype=f)
            d["nab%d" % l] = na_bias_tiles(inp["na_rpb"][l][2 * q:2 * q + 2], tiles)
            wo = inp["w_out"][l]
            d["wo%d" % l] = np.ascontiguousarray(np.concatenate(
                [wo[blk * 512 + 128 * i:blk * 512 + 128 * (i + 1)] for i in range(4) for blk in range(3)], axis=0))
        d["fg"] = np.ascontiguousarray(np.broadcast_to(inp["final_norm_g"][None, :], (128, 1024)), dtype=f)
        per.append(d)
    return per


_PROGS = {}


def get_prog(nreal, phases):
    key = (nreal, tuple(phases))
    if key not in _PROGS:
        _PROGS[key] = build(nreal, list(phases))
    return _PROGS[key]


M_IN = ["cmask", "w", "gbc", "pv", "wup", "aup", "gup", "vdn", "vup", "nab"]


def in_names(phases):
    has = set(phases)
    names = ["cmask"]
    for l in range(2):
        names += ["%s%d" % (k, l) for k in M_IN[1:]]
    if "M0" in has:
        names.append("hg0")
    for l in range(2):
        if "O%d" % l in has:
            names.append("wo%d" % l)
    if "O0" in has:
        names.append("hmine")
    if "O1" in has:
        names.append("fg")
    return names


FUSED = False
NREAL = 128


def kernel(**inp):
    inp = {k: np.asarray(v) for k, v in inp.items()}
    nreal = NREAL
    npq, ntl = nreal // 4, nreal // 4 + 1
    per = host_layouts(inp, nreal)
    B = inp["x"].shape[0]
    out = np.empty((B, nreal * 128, 1024), np.float32)
    cores = list(range(8))
    if FUSED:
        ph = ["M0", "O0", "M1", "O1"]
        C = get_prog(nreal, ph)
        maps = [{k: per[c][k] for k in in_names(ph)} for c in cores]
        res = run_bass_kernel_spmd(C.nc, maps, core_ids=cores).results
    else:
        def run(ph, extra):
            C = get_prog(nreal, ph)
            maps = []
            for c in cores:
                m = {k: per[c][k] for k in in_names(ph)}
                m.update(extra[c])
                maps.append(m)
            return run_bass_kernel_spmd(C.nc, maps, core_ids=cores).results

        def a2a(r, name):
            og = [np.asarray(r[c][name]).reshape(4, 384, ntl * 128) for c in cores]
            return [np.ascontiguousarray(np.stack([og[4 * (c // 4) + i][c % 4] for i in range(4)])).reshape(4 * 384, ntl * 128) for c in cores]

        r = run(["M0"], [{} for _ in cores])
        vf = [np.asarray(r[c]["vfirst_o"]) for c in cores]
        ogo = a2a(r, "ogi0")
        r = run(["O0"], [{"ogo0": ogo[c]} for c in cores])
        hgi = [np.asarray(r[c]["hgi"]) for c in cores]
        hg1 = [np.ascontiguousarray(np.concatenate([hgi[4 * (c // 4) + i] for i in range(4)], axis=0)) for c in cores]
        r = run(["M1"], [{"hg1": hg1[c], "vfirst_i": vf[c]} for c in cores])
        ogo = a2a(r, "ogi1")
        res = run(["O1"], [{"ogo1": ogo[c], "hgi": hgi[c]} for c in cores])
    for c in cores:
        b, q = c // 4, c % 4
        out[b, q * npq * 128:(q + 1) * npq * 128] = np.asarray(res[c]["out"])
    return out
```
